# Optimizing a Trainium2 kernel written in Bass

```python
import math
import jax
import jax.numpy as jnp
from jax import lax
import numpy as np

D_MODEL = 2048
BATCH = 2
SEQ = 4096
DEPTH = 2

GRID_W = 64
CTX_LEN = 256
HEAD_DIM = 128
GDN_HEADS = 8
NA_HEADS = 8
GDN_WIDTH = GDN_HEADS * HEAD_DIM
NA_WIDTH = NA_HEADS * HEAD_DIM
GDN_CONV = 5
GDN_CHUNK = 64
NA_KH = 8
NA_KW = 16
SC_CONV = 3
FFN_HIDDEN = -(-8 * D_MODEL // (3 * 256)) * 256
ROPE_THETA = 10000.0
NORM_EPS = 1e-6
EVEN_SEGS = (GDN_WIDTH, GDN_WIDTH, NA_WIDTH, NA_WIDTH, 2 * GDN_HEADS, 2 * GDN_HEADS, GDN_WIDTH, NA_WIDTH, GDN_WIDTH)
N_CTX_SEGS = 6

kernel_name = 'hybrid_gdn_natten_shortconv_dit'


def rmsnorm(x, g):
    xf = x.astype(jnp.float32)
    y = xf * lax.rsqrt(jnp.mean(xf * xf, axis=-1, keepdims=True) + NORM_EPS)
    return (y * g.astype(jnp.float32)).astype(x.dtype)


def modulate(x, g, shift, scale):
    return rmsnorm(x, g) * (1 + scale[..., None, :]) + shift[..., None, :]


def l2norm(x):
    xf = x.astype(jnp.float32)
    return xf * lax.rsqrt(jnp.sum(xf * xf, axis=-1, keepdims=True) + NORM_EPS)


def heads(t):
    return t.reshape(t.shape[0], t.shape[1], -1, HEAD_DIM)


def to_bhtd(t):
    return None if t is None else jnp.swapaxes(t, 1, 2)


def split_segments(p, sizes):
    cuts = [int(s) for s in np.cumsum(sizes)[:-1]]
    return jnp.split(p, cuts, axis=-1)


def depthwise_conv_centred(x, w):
    k = w.shape[0]
    return lax.conv_general_dilated(
        x, w[:, None, :].astype(x.dtype), window_strides=(1,), padding=[(k // 2, k // 2)],
        dimension_numbers=('NWC', 'WIO', 'NWC'), feature_group_count=x.shape[-1])


def axial_rope(x):
    t = jnp.arange(x.shape[1])
    row = (t // GRID_W).astype(jnp.float32)
    col = (t % GRID_W).astype(jnp.float32)
    n_freq = HEAD_DIM // 4
    inv_freq = ROPE_THETA ** (-jnp.arange(n_freq, dtype=jnp.float32) / n_freq)
    ang = jnp.concatenate([row[:, None] * inv_freq, col[:, None] * inv_freq], axis=-1)[:, None, :]
    cos, sin = jnp.cos(ang), jnp.sin(ang)
    x1, x2 = jnp.split(x, 2, axis=-1)
    return jnp.concatenate([x1 * cos - x2 * sin, x1 * sin + x2 * cos], axis=-1)


def gated_delta_chunked(q, k, v, g, beta, s0):
    f32 = jnp.float32
    k, v, g, beta = (t.astype(f32) for t in (k, v, g, beta))
    bn, hn, tn, _ = k.shape
    n, cs = tn // GDN_CHUNK, GDN_CHUNK

    def chunks(t):
        return t.reshape(t.shape[:2] + (n, cs) + t.shape[3:])

    k, v, g, beta = chunks(k), chunks(v), chunks(g), chunks(beta)
    gc = jnp.cumsum(g, axis=-1)
    i = jnp.arange(cs)
    lower_incl = i[:, None] >= i[None, :]
    decay = jnp.exp(jnp.where(lower_incl, gc[..., :, None] - gc[..., None, :], -jnp.inf))
    kb = k * beta[..., None]
    m = jnp.where(i[:, None] > i[None, :], jnp.einsum('bhnid,bhnjd->bhnij', kb, k) * decay, 0.0)
    eye = jnp.eye(cs, dtype=f32)
    t_inv = lax.linalg.triangular_solve(eye + m, jnp.broadcast_to(eye, m.shape),
                                        left_side=True, lower=True, unit_diagonal=True)
    u = jnp.einsum('bhnij,bhnjd->bhnid', t_inv, v * beta[..., None])
    w = jnp.einsum('bhnij,bhnjd->bhnid', t_inv, kb * jnp.exp(gc)[..., None])
    g_last = gc[..., -1]
    k_dec = k * jnp.exp(g_last[..., None] - gc)[..., None]

    def to_scan(t):
        return jnp.moveaxis(t, 2, 0)

    xs = [to_scan(w), to_scan(u), to_scan(k_dec), to_scan(g_last)]
    if q is not None:
        q = chunks(q.astype(f32))
        a_intra = jnp.where(lower_incl, jnp.einsum('bhnid,bhnjd->bhnij', q, k) * decay, 0.0)
        xs += [to_scan(q * jnp.exp(gc)[..., None]), to_scan(a_intra)]

    def step(s, xc):
        w_c, u_c, kd_c, gl_c = xc[:4]
        v_new = u_c - jnp.einsum('bhik,bhkv->bhiv', w_c, s)
        s_next = s * jnp.exp(gl_c)[..., None, None] + jnp.einsum('bhik,bhiv->bhkv', kd_c, v_new)
        if len(xc) == 4:
            return s_next, None
        qd_c, a_c = xc[4:]
        o_c = jnp.einsum('bhik,bhkv->bhiv', qd_c, s) + jnp.einsum('bhij,bhjv->bhiv', a_c, v_new)
        return s_next, o_c

    s_fin, o = lax.scan(step, s0, tuple(xs))
    if q is None:
        return None, s_fin
    return jnp.moveaxis(o, 0, 2).reshape(bn, hn, tn, -1), s_fin


def flip_time(t, d):
    return t if (t is None or d == 0) else jnp.flip(t, axis=2)


def gdn_bidirectional(q, k, v, g, beta, init):
    out, finals = None, []
    for d in range(2):
        o_d, s_d = gated_delta_chunked(flip_time(q, d), flip_time(k, d), flip_time(v, d),
                                       flip_time(g[d], d), flip_time(beta[d], d), init[d])
        if o_d is not None:
            o_d = flip_time(o_d, d)
            out = o_d if out is None else out + o_d
        finals.append(s_d)
    return out, finals


def gdn_gates(a, b, a_log, dt_bias):
    shp = a.shape[:2] + (2, GDN_HEADS)
    a = a.astype(jnp.float32).reshape(shp)
    b = b.astype(jnp.float32).reshape(shp)
    g = -jnp.exp(a_log.astype(jnp.float32)) * jax.nn.softplus(a + dt_bias.astype(jnp.float32))
    beta = jax.nn.sigmoid(b)
    return jnp.transpose(g, (2, 0, 3, 1)), jnp.transpose(beta, (2, 0, 3, 1))


def gdn_out_norm(o, z, g_norm):
    o = jnp.swapaxes(o, 1, 2)
    y = rmsnorm(o, g_norm) * jax.nn.silu(heads(z).astype(jnp.float32))
    return y.reshape(y.shape[0], y.shape[1], GDN_WIDTH).astype(z.dtype)


def short_conv_silu(parts, conv_w):
    xcat = jnp.concatenate(parts, axis=-1)
    y = jax.nn.silu(depthwise_conv_centred(xcat, conv_w[:, :xcat.shape[-1]]))
    return jnp.split(y, len(parts), axis=-1)


def neighbourhood_attention(q, k, v, k_ctx, v_ctx, rpb):
    bn, tn, hn, dh = q.shape
    rows = tn // GRID_W
    kh = min(NA_KH, rows)
    r = jnp.arange(rows)
    row_idx = jnp.clip(r - kh // 2, 0, rows - kh)[:, None] + jnp.arange(kh)[None, :]
    col = jnp.arange(GRID_W)
    col_start = jnp.clip(col - NA_KW // 2, 0, GRID_W - NA_KW)
    in_win = (col[None, :] >= col_start[:, None]) & (col[None, :] < col_start[:, None] + NA_KW)
    dr = row_idx - r[:, None] + (NA_KH - 1)
    dc = jnp.clip(col[None, :] - col[:, None], -(NA_KW - 1), NA_KW - 1) + (NA_KW - 1)
    bias = rpb.astype(jnp.float32)[:, dr[:, None, :, None], dc[None, :, None, :]]
    scale = dh ** -0.5
    qg = q.reshape(bn, rows, GRID_W, hn, dh)
    kg = k.reshape(bn, rows, GRID_W, hn, dh)[:, row_idx]
    vg = v.reshape(bn, rows, GRID_W, hn, dh)[:, row_idx]
    s_loc = jnp.einsum('brqhd,brjkhd->bhrqjk', qg, kg).astype(jnp.float32) * scale + bias[None]
    s_loc = jnp.where(in_win[:, None, :], s_loc, -jnp.inf)
    n_loc = kh * GRID_W
    s_loc = s_loc.reshape(bn, hn, rows, GRID_W, n_loc)
    s_ctx = jnp.einsum('brqhd,bchd->bhrqc', qg, k_ctx).astype(jnp.float32) * scale
    p = jax.nn.softmax(jnp.concatenate([s_loc, s_ctx], axis=-1), axis=-1)
    p_loc = p[..., :n_loc].reshape(bn, hn, rows, GRID_W, kh, GRID_W).astype(v.dtype)
    p_ctx = p[..., n_loc:].astype(v.dtype)
    o = jnp.einsum('bhrqjk,brjkhd->brqhd', p_loc, vg) + jnp.einsum('bhrqc,bchd->brqhd', p_ctx, v_ctx)
    return o.reshape(bn, tn, hn * dh)


def context_attention(q, k, v):
    s = jnp.einsum('bqhd,bkhd->bhqk', q, k).astype(jnp.float32) * HEAD_DIM ** -0.5
    p = jax.nn.softmax(s, axis=-1).astype(v.dtype)
    o = jnp.einsum('bhqk,bkhd->bqhd', p, v)
    return o.reshape(o.shape[0], o.shape[1], -1)


def even_mixer(h, hc, w_in, conv_w, a_log, dt_bias, gdn_norm, rpb, w_out, want_ctx):
    f32 = jnp.float32
    bn = h.shape[0]
    n_seg_c = len(EVEN_SEGS) if want_ctx else N_CTX_SEGS
    pc = hc @ w_in[:, :sum(EVEN_SEGS[:n_seg_c])]
    segs_c = split_segments(pc, EVEN_SEGS[:n_seg_c])
    ka_c, va_c, kb_c, vb_c, a_c, b_c = segs_c[:6]
    conv_c = short_conv_silu([ka_c, va_c] + ([segs_c[6]] if want_ctx else []), conv_w)
    k_c = l2norm(heads(conv_c[0]))
    v_c = heads(conv_c[1]).astype(f32)
    q_c = l2norm(heads(conv_c[2])) * HEAD_DIM ** -0.5 if want_ctx else None
    g_c, beta_c = gdn_gates(a_c, b_c, a_log, dt_bias)
    zeros = jnp.zeros((bn, GDN_HEADS, HEAD_DIM, HEAD_DIM), f32)
    o_c, s_ctx_fin = gdn_bidirectional(to_bhtd(q_c), to_bhtd(k_c), to_bhtd(v_c), g_c, beta_c, (zeros, zeros))
    k_ctx_na, v_ctx_na = heads(kb_c), heads(vb_c)
    ka, va, kb, vb, a, b, qa, qb, za = split_segments(h @ w_in, EVEN_SEGS)
    ka, va, qa = short_conv_silu([ka, va, qa], conv_w)
    k_l = axial_rope(l2norm(heads(ka)))
    q_l = axial_rope(l2norm(heads(qa))) * HEAD_DIM ** -0.5
    g_l, beta_l = gdn_gates(a, b, a_log, dt_bias)
    o_l, _ = gdn_bidirectional(to_bhtd(q_l), to_bhtd(k_l), to_bhtd(heads(va)), g_l, beta_l, s_ctx_fin)
    y_gdn = gdn_out_norm(o_l, za, gdn_norm).astype(h.dtype)
    y_na = neighbourhood_attention(heads(qb), heads(kb), heads(vb), k_ctx_na, v_ctx_na, rpb)
    y = jnp.concatenate([y_gdn, y_na], axis=-1) @ w_out
    if not want_ctx:
        return y, None
    yc_gdn = gdn_out_norm(o_c, segs_c[8], gdn_norm).astype(hc.dtype)
    yc_na = context_attention(heads(segs_c[7]), k_ctx_na, v_ctx_na)
    yc = jnp.concatenate([yc_gdn, yc_na], axis=-1) @ w_out
    return y, yc


def shortconv_mixer(h, w_in, conv_w, w_out):
    gate_b, gate_c, val = jnp.split(h @ w_in, 3, axis=-1)
    return (gate_b * depthwise_conv_centred(gate_c * val, conv_w)) @ w_out


def swiglu(h, w_gate, w_up, w_down):
    return (jax.nn.silu(h @ w_gate) * (h @ w_up)) @ w_down


def setup_inputs(seed: int = 0) -> dict:
    key = jax.random.key(seed)
    ks = iter(jax.random.split(key, 32))
    n_even, n_odd = (DEPTH + 1) // 2, DEPTH // 2
    f32 = jnp.float32
    d = D_MODEL

    def normal(shape, scale):
        return jax.random.normal(next(ks), shape, f32) * scale

    def gain(shape):
        return 1.0 + normal(shape, 0.02)

    dt = jnp.exp(jax.random.uniform(next(ks), (n_even, 2, GDN_HEADS), f32,
                                    minval=math.log(1e-3), maxval=math.log(1e-1)))
    a_log = jnp.log(jax.random.uniform(next(ks), (n_even, 2, GDN_HEADS), f32, minval=1.0, maxval=16.0))
    mix_w = GDN_WIDTH + NA_WIDTH
    return {
        'x': normal((BATCH, SEQ, d), 1.0),
        'c': normal((BATCH, d), 1.0),
        'ctx': normal((BATCH, CTX_LEN, d), 1.0),
        'c_ctx': normal((d,), 1.0),
        'ada_w': normal((DEPTH, d, 6 * d), 0.5 * d ** -0.5),
        'ada_b': normal((DEPTH, 6 * d), 0.02),
        'norm_mix': gain((DEPTH, d)),
        'norm_ffn': gain((DEPTH, d)),
        'ffn_w_gate': normal((DEPTH, d, FFN_HIDDEN), d ** -0.5),
        'ffn_w_up': normal((DEPTH, d, FFN_HIDDEN), d ** -0.5),
        'ffn_w_down': normal((DEPTH, FFN_HIDDEN, d), FFN_HIDDEN ** -0.5),
        'final_norm': gain((d,)),
        'ev_w_in': normal((n_even, d, sum(EVEN_SEGS)), d ** -0.5),
        'ev_conv': normal((n_even, GDN_CONV, 3 * GDN_WIDTH), GDN_CONV ** -0.5),
        'ev_a_log': a_log,
        'ev_dt_bias': dt + jnp.log(-jnp.expm1(-dt)),
        'ev_gdn_norm': gain((n_even, HEAD_DIM)),
        'ev_rpb': normal((n_even, NA_HEADS, 2 * NA_KH - 1, 2 * NA_KW - 1), 0.1),
        'ev_w_out': normal((n_even, mix_w, d), mix_w ** -0.5),
        'od_w_in': normal((n_odd, d, 3 * d), d ** -0.5),
        'od_conv': normal((n_odd, SC_CONV, d), SC_CONV ** -0.5),
        'od_w_out': normal((n_odd, d, d), d ** -0.5),
    }


def reference(x, c, ctx, c_ctx, ada_w, ada_b, norm_mix, norm_ffn, ffn_w_gate, ffn_w_up, ffn_w_down,
              final_norm, ev_w_in, ev_conv, ev_a_log, ev_dt_bias, ev_gdn_norm, ev_rpb, ev_w_out,
              od_w_in, od_conv, od_w_out):
    x_lat, x_ctx = x, ctx
    for l in range(DEPTH):
        ctx_read_later = any(j % 2 == 0 for j in range(l + 1, DEPTH))
        need_ctx = (l % 2 == 0) or ctx_read_later
        sh1, sc1, g1, sh2, sc2, g2 = jnp.split(jax.nn.silu(c) @ ada_w[l] + ada_b[l], 6, axis=-1)
        h = modulate(x_lat, norm_mix[l], sh1, sc1)
        hc = None
        if need_ctx:
            csh1, csc1, cg1, csh2, csc2, cg2 = jnp.split(jax.nn.silu(c_ctx) @ ada_w[l] + ada_b[l], 6, axis=-1)
            hc = modulate(x_ctx, norm_mix[l], csh1, csc1)
        if l % 2 == 0:
            e = l // 2
            y, yc = even_mixer(h, hc, ev_w_in[e], ev_conv[e], ev_a_log[e], ev_dt_bias[e],
                               ev_gdn_norm[e], ev_rpb[e], ev_w_out[e], ctx_read_later)
        else:
            j_odd = l // 2
            y = shortconv_mixer(h, od_w_in[j_odd], od_conv[j_odd], od_w_out[j_odd])
            yc = shortconv_mixer(hc, od_w_in[j_odd], od_conv[j_odd], od_w_out[j_odd]) if ctx_read_later else None
        x_lat = x_lat + g1[:, None, :] * y
        x_lat = x_lat + g2[:, None, :] * swiglu(modulate(x_lat, norm_ffn[l], sh2, sc2),
                                                ffn_w_gate[l], ffn_w_up[l], ffn_w_down[l])
        if ctx_read_later:
            x_ctx = x_ctx + cg1 * yc
            x_ctx = x_ctx + cg2 * swiglu(modulate(x_ctx, norm_ffn[l], csh2, csc2),
                                         ffn_w_gate[l], ffn_w_up[l], ffn_w_down[l])
    return rmsnorm(x_lat, final_norm)
```

```python
import os
import numpy as np
import ml_dtypes
from contextlib import ExitStack
import concourse.bass as bass
import concourse.mybir as mybir
from concourse.bass_utils import run_bass_kernel_spmd


F32 = mybir.dt.float32
BF16 = mybir.dt.bfloat16
AF = mybir.ActivationFunctionType
ALU = mybir.AluOpType
AX = mybir.AxisListType
R32 = mybir.dt.float32r
DTSIZE = {F32: 4, BF16: 2, R32: 4}

ENGS = ("pe", "act", "dve", "pool", "sp")
EPOCH = 12000
NDSEM = {"sp": 16, "pool": 8, "act": 8}
DQ = ("sp", "pool", "act")


def ap_box(ap):
    es = DTSIZE[ap.dtype]
    pat = ap.ap
    off = ap.offset
    name = ap.tensor.name
    if str(ap.space) == "DRAM":
        hi = off
        for (st, cnt) in pat:
            hi += st * (cnt - 1)
        return (name, 0, 1, off * es, (hi + 1) * es)
    row = pat[0][0]
    if row == 0:
        row = 1 << 40
    p0 = off // row
    f0 = off % row
    p1 = p0 + pat[0][1]
    hi = f0
    for (st, cnt) in pat[1:]:
        hi += st * (cnt - 1)
    if str(ap.space) == "PSUM":
        return (name, (p0 // 32) * 32, ((p1 + 31) // 32) * 32, 0, 1 << 20)
    return (name, p0, p1, f0 * es, (hi + 1) * es)


class Prog:
    def __init__(self, nc):
        self.nc = nc
        self.ops = {e: [] for e in ENGS}
        self.cnt = {e: 0 for e in ENGS if e != "sp"}
        self.seen = {e: {} for e in ENGS}
        self.recs = {}
        self.ndma = {q: 0 for q in DQ}
        self.dma_tok = {q: [None] * NDSEM[q] for q in DQ}

    @staticmethod
    def _ov(a, b):
        return a[1] < b[2] and b[1] < a[2] and a[3] < b[4] and b[3] < a[4]

    @staticmethod
    def _covers(a, b):
        return a[1] <= b[1] and a[2] >= b[2] and a[3] <= b[3] and a[4] >= b[4]

    def _deps(self, reads, writes):
        deps = []
        for ap in reads:
            b = ap_box(ap)
            psum = str(ap.space) == "PSUM"
            for (ob, tok, isw) in self.recs.get(b[0], ()):
                if (isw or psum) and self._ov(b, ob):
                    deps.append(tok)
        for ap in writes:
            b = ap_box(ap)
            for (ob, tok, isw) in self.recs.get(b[0], ()):
                if self._ov(b, ob):
                    deps.append(tok)
        return deps

    def _record(self, reads, writes, tok):
        for ap in writes:
            b = ap_box(ap)
            lst = self.recs.setdefault(b[0], [])
            lst[:] = [r for r in lst if not self._covers(b, r[0])]
            lst.append([b, tok, True])
        for ap in reads:
            b = ap_box(ap)
            lst = self.recs.setdefault(b[0], [])
            if tok[0] == "e":
                lst[:] = [r for r in lst if not ((not r[2]) and r[1][0] == "e" and r[1][1] == tok[1]
                                                 and self._covers(b, r[0]))]
            lst.append([b, tok, False])

    def _waits(self, eng, deps, skip_same=False):
        need = {}
        for tok in deps:
            if tok[0] == "e":
                if tok[1] == eng and skip_same:
                    continue
                key = ("e", tok[1])
            else:
                key = ("d", tok[1])
            v = tok[2]
            if self.seen[eng].get(key, 0) >= v:
                continue
            if need.get(key, 0) < v:
                need[key] = v
        for k, v in need.items():
            self.seen[eng][k] = v
        return list(need.items())

    def op(self, eng, fn, reads=(), writes=(), skip_same=False):
        deps = self._deps(reads, writes)
        waits = self._waits(eng, deps, skip_same)
        self.cnt[eng] += 1
        idx = self.cnt[eng]
        tok = ("e", eng, idx)
        self._record(reads, writes, tok)
        self.ops[eng].append(("op", fn, waits, idx))
        return tok

    def dma(self, queue, out, in_):
        deps = self._deps([in_], [out])
        ns = NDSEM[queue]
        slot = self.ndma[queue] % ns
        val = 16 * (self.ndma[queue] // ns + 1)
        self.ndma[queue] += 1
        prev = self.dma_tok[queue][slot]
        if prev is not None:
            deps.append(prev)
        waits = self._waits(queue, deps)
        tok = ("d", (queue, slot), val)
        self.dma_tok[queue][slot] = tok
        self._record([in_], [out], tok)
        self.ops[queue].append(("dma", (out, in_), waits, ((queue, slot), val)))
        return tok

    def barrier(self):
        deps = [("e", e, self.cnt[e]) for e in self.cnt if self.cnt[e] > 0]
        deps += [t for q in DQ for t in self.dma_tok[q] if t is not None]
        for eng in ENGS:
            waits = self._waits(eng, deps, skip_same=True)
            if waits:
                self.ops[eng].append(("wait", None, waits, None))
        self.recs = {}

    def emit(self):
        nc = self.nc
        with ExitStack() as es:
            esem = {}
            for e in self.cnt:
                n_ep = self.cnt[e] // EPOCH + 1
                esem[e] = [es.enter_context(nc.semaphore(f"s_{e}_{k}")) for k in range(n_ep)]
            dsem = {(q, k): es.enter_context(nc.semaphore(f"s_dma_{q}_{k}")) for q in DQ for k in range(NDSEM[q])
                    if self.ndma[q] > k}
            block = es.enter_context(nc.Block())

            def semval(key, v):
                if key[0] == "e":
                    ep = (v - 1) // EPOCH
                    return esem[key[1]][ep], v - ep * EPOCH
                return dsem[key[1]], v

            def run(engname, engobj):
                for (kind, payload, waits, info) in self.ops[engname]:
                    for (key, v) in waits:
                        s, sv = semval(key, v)
                        engobj.wait_ge(s, sv)
                    if kind == "op":
                        ins = payload(engobj)
                        s, sv = semval(("e", engname), info)
                        ins.then_inc(s, 1)
                    elif kind == "dma":
                        out, in_ = payload
                        ins = engobj.dma_start(out=out, in_=in_)
                        ins.then_inc(dsem[info[0]], 16)
                if engname == "sp":
                    for q in DQ:
                        for tok in self.dma_tok[q]:
                            if tok is not None:
                                engobj.wait_ge(dsem[tok[1]], tok[2])
                    for e in self.cnt:
                        if self.cnt[e] > 0:
                            s, sv = semval(("e", e), self.cnt[e])
                            engobj.wait_ge(s, sv)

            @block.tensor
            def _(eng):
                run("pe", eng)

            @block.scalar
            def _(eng):
                run("act", eng)

            @block.vector
            def _(eng):
                run("dve", eng)

            @block.gpsimd
            def _(eng):
                run("pool", eng)

            @block.sync
            def _(eng):
                run("sp", eng)


def _isap(x):
    return not isinstance(x, (int, float)) and x is not None


class KB:
    def __init__(self, nc, es):
        self.nc = nc
        self.es = es
        self.p = Prog(nc)
        self.banks = [es.enter_context(nc.psum_tensor(f"psb{i}", [128, 512], F32)) for i in range(6)]
        self.bbanks = [es.enter_context(nc.psum_tensor(f"psh{i}", [128, 1024], BF16)) for i in range(2)]
        self.bi = 0
        self.bbi = 0
        self.pools = {}
        self.scope = None
        self.outer = None
        self.nname = 0

    def bank(self):
        b = self.banks[self.bi % len(self.banks)]
        self.bi += 1
        return b

    def bbank(self):
        b = self.bbanks[self.bbi % len(self.bbanks)]
        self.bbi += 1
        return b

    def sb(self, name, shape, dt=F32):
        es = self.scope if self.scope is not None else (self.outer if self.outer is not None else self.es)
        self.nname += 1
        return es.enter_context(self.nc.sbuf_tensor(f"{name}_{self.nname}", list(shape), dt))

    def tmp(self, pool, shape, dt=F32, nbuf=2):
        if pool not in self.pools:
            self.pools[pool] = ([self.sb(f"{pool}{i}", shape, dt) for i in range(nbuf)], [0])
        bufs, ctr = self.pools[pool]
        t = bufs[ctr[0] % len(bufs)]
        ctr[0] += 1
        return t

    def open_outer(self):
        assert self.outer is None and self.scope is None
        self.outer = ExitStack()
        self.outer.__enter__()

    def close_outer(self):
        assert self.scope is None
        self.p.barrier()
        self.outer.__exit__(None, None, None)
        self.outer = None

    def open_scope(self):
        assert self.scope is None
        self.scope = ExitStack()
        self.scope.__enter__()
        self.pools = {}

    def close_scope(self):
        self.p.barrier()
        self.scope.__exit__(None, None, None)
        self.scope = None
        self.pools = {}

    def mm(self, out, lhsT, rhs, start=True, stop=True):
        self.p.op("pe", lambda e: e.matmul(out, lhsT=lhsT, rhs=rhs, start=start, stop=stop),
                  reads=[lhsT, rhs], writes=[out], skip_same=True)

    def tr(self, out, in_, ident):
        self.p.op("pe", lambda e: e.transpose(out, in_, ident), reads=[in_, ident], writes=[out], skip_same=True)

    def act(self, out, in_, func, bias=None, scale=None, accum_out=None):
        kw = {}
        reads = [in_]
        writes = [out]
        if bias is not None:
            kw["bias"] = bias
            if _isap(bias):
                reads.append(bias)
        if scale is not None:
            kw["scale"] = scale
            if _isap(scale):
                reads.append(scale)
        if accum_out is not None:
            kw["accum_out"] = accum_out
            writes.append(accum_out)
        self.p.op("act", lambda e: e.activation(out=out, in_=in_, func=func, **kw), reads=reads, writes=writes)

    def tt(self, eng, out, in0, in1, op):
        self.p.op(eng, lambda e: e.tensor_tensor(out=out, in0=in0, in1=in1, op=op), reads=[in0, in1], writes=[out])

    def ts(self, eng, out, in0, s1, op0, s2=None, op1=None):
        reads = [in0] + [s for s in (s1, s2) if _isap(s)]
        if op1 is None:
            self.p.op(eng, lambda e: e.tensor_scalar(out=out, in0=in0, scalar1=s1, scalar2=None, op0=op0),
                      reads=reads, writes=[out])
        else:
            self.p.op(eng, lambda e: e.tensor_scalar(out=out, in0=in0, scalar1=s1, scalar2=s2, op0=op0, op1=op1),
                      reads=reads, writes=[out])

    def stt(self, eng, out, in0, scalar, in1, op0, op1):
        reads = [in0, in1] + ([scalar] if _isap(scalar) else [])
        self.p.op(eng, lambda e: e.scalar_tensor_tensor(out=out, in0=in0, scalar=scalar, in1=in1, op0=op0, op1=op1),
                  reads=reads, writes=[out])

    def copy(self, eng, out, in_):
        if eng == "act":
            self.p.op("act", lambda e: e.copy(out=out, in_=in_), reads=[in_], writes=[out])
        else:
            self.p.op(eng, lambda e: e.tensor_copy(out=out, in_=in_), reads=[in_], writes=[out])

    def memset(self, eng, ap, val):
        self.p.op(eng, lambda e: e.memset(ap, val), reads=[], writes=[ap])

    def recip(self, out, in_):
        self.p.op("dve", lambda e: e.reciprocal(out=out, in_=in_), reads=[in_], writes=[out])

    def rmax(self, out, in_):
        self.p.op("dve", lambda e: e.reduce_max(out=out, in_=in_, axis=AX.X), reads=[in_], writes=[out])

    def dma(self, q, out, in_):
        self.p.dma(q, out, in_)

STOP = os.environ.get('STOP_AFTER', '')
TMN = int(os.environ.get('TMN', '264'))

T = 4096
L = 256
D = 2048
NCH = 16
EPS = 1e-6
SCALE = 128 ** -0.5

M_UF, M_UB, M_SELF, M_SELB, M_R63, M_R127, M_R0, M_R64, M_AIF, M_ASF, M_AIB, M_ASB, M_ROT = range(13)
NMASK = 13


def phase_a_io(nc):
    io = {}

    def I(name, shape, dt=F32):
        io[name] = nc.dram_tensor(name, list(shape), dt, kind="ExternalInput").ap()

    I("xT", [D, T]); I("ctxT", [D, L]); I("modA", [128, 2, NCH, 2])
    I("nmix", [128, NCH]); I("wf", [D, 1536]); I("wt", [D, 264]); I("convw", [128, 3, 2, 5]); I("gpar", [128, 8])
    I("gnorm", [128, 1]); I("nabias", [5, 2, 128, 640]); I("ident", [128, 128]); I("cmask", [128, NMASK, 128])
    I("cos", [128, T]); I("sin", [128, T])
    io["ymix"] = nc.dram_tensor("ymix", [4, 128, T], BF16, kind="ExternalOutput").ap()
    return io


def gate_core(kb, masks, gsrc, NT, tag, gp_, nega_):
    G = kb.sb("G" + tag, [128, NT, 8])
    kb.dma("sp", G[:], gsrc)
    t1 = kb.sb("t1" + tag, [128, 4, NT])
    kb.tt("dve", t1[:], G[:, :, 0:4].rearrange("p n c -> p c n"),
          gp_[:, 4:8].unsqueeze(2).to_broadcast([128, 4, NT]), ALU.add)
    kb.act(t1[:], t1[:], AF.Exp)
    kb.act(t1[:], t1[:], AF.Ln, bias=1.0)
    g = kb.sb("g" + tag, [128, 4, NT])
    kb.tt("dve", g[:], t1[:], nega_[:].unsqueeze(2).to_broadcast([128, 4, NT]), ALU.mult)
    gcps = kb.bank()
    kb.mm(gcps[:, 0:2 * NT], masks[:, M_UF, :], g[:, 0:2, :].rearrange("p c n -> p (c n)"))
    kb.mm(gcps[:, 2 * NT:4 * NT], masks[:, M_UB, :], g[:, 2:4, :].rearrange("p c n -> p (c n)"))
    gc = kb.sb("gc" + tag, [128, 4, NT])
    kb.copy("dve", gc[:].rearrange("p c n -> p (c n)"), gcps[:, 0:4 * NT])
    bcps = kb.bank()
    for q, (rf, rb) in enumerate(((M_R63, M_R0), (M_R127, M_R64))):
        kb.mm(bcps[:, (q * 4) * NT:(q * 4 + 2) * NT], masks[:, rf, :], gc[:, 0:2, :].rearrange("p c n -> p (c n)"))
        kb.mm(bcps[:, (q * 4 + 2) * NT:(q * 4 + 4) * NT], masks[:, rb, :], gc[:, 2:4, :].rearrange("p c n -> p (c n)"))
    egl = kb.sb("egl" + tag, [128, 2, 4, NT])
    kb.act(egl[:].rearrange("p q c n -> p (q c n)"), bcps[:, 0:8 * NT], AF.Exp)
    return G, gc, egl


def phase_a_shared(nc, kb, io, ng=1):
    sh = {}

    def scr(name, shape, dt=F32):
        return nc.dram_tensor(name, list(shape), dt).ap()

    sh["ng"] = ng
    sh["fm32"] = scr("fm32", [ng, 8, 128, T])
    sh["fm32c"] = scr("fm32c", [4, 128, L])
    sh["fmbf"] = scr("fmbf", [4, 128, T], BF16)
    sh["fmbfc"] = scr("fmbfc", [2, 128, L], BF16)
    sh["vbs"] = scr("vbs", [128, 32, 256], BF16)
    sh["vbsc"] = scr("vbsc", [128, 2, 256], BF16)
    sh["gts"] = scr("gts", [ng, 128, 32, 8])
    sh["gtsc"] = scr("gtsc", [ng, 128, 2, 8])
    sh["spill"] = scr("spill", [ng, 4, 32, 5, 128, 128])
    sh["spillc"] = scr("spillc", [ng, 4, 2, 5, 128, 128])
    ident = kb.sb("ident", [128, 128])
    kb.dma("sp", ident[:], io["ident"])
    identb = kb.sb("identb", [128, 128], BF16)
    kb.dma("pool", identb[:], io["ident"])
    ones = kb.sb("ones", [128, 128])
    kb.memset("dve", ones[:], 1.0)
    onesb = kb.sb("onesb", [128, 128], BF16)
    kb.memset("dve", onesb[:], 1.0)
    masks = kb.sb("masks", [128, NMASK, 128])
    kb.dma("sp", masks[:], io["cmask"])
    onesr = kb.sb("onesr", [128, 128], R32)
    kb.ts("dve", onesr[:], ident[:], 0.0, ALU.mult, 1.0, ALU.add)
    masksr = kb.sb("masksr", [128, NMASK, 128], R32)
    kb.copy("dve", masksr[:], masks[:])
    sh.update(ident=ident, identb=identb, ones=ones, onesb=onesb, masks=masks, onesr=onesr, masksr=masksr)
    sh["A1v"] = kb.sb("A1v", [128, NCH, 2])
    sh["B1v"] = kb.sb("B1v", [128, NCH, 2])
    return sh


def phase_a(nc, kb, io, dbg=None, sh=None, g=None, modsrc=None, ymix_dst=None, do_scan=True):
    p = kb.p
    if sh is None:
        sh = phase_a_shared(nc, kb, io)
    gidx = 0 if (g is None or sh["ng"] == 1) else g
    fm32, fm32c, fmbf, fmbfc = sh["fm32"][gidx], sh["fm32c"], sh["fmbf"], sh["fmbfc"]
    vbs, vbsc, gts, gtsc, spill, spillc = sh["vbs"], sh["vbsc"], sh["gts"][gidx], sh["gtsc"][gidx], sh["spill"][gidx], sh["spillc"][gidx]
    ident, identb, ones, onesb, masks = sh["ident"], sh["identb"], sh["ones"], sh["onesb"], sh["masks"]
    onesr, masksr = sh["onesr"], sh["masksr"]
    A1v, B1v = sh["A1v"], sh["B1v"]
    W = (lambda name: io[name]) if g is None else (lambda name: io[name][g])
    ymix = io["ymix"] if ymix_dst is None else ymix_dst

    if modsrc is None:
        kb.open_scope()
        modA = kb.sb("modA", [128, 2, NCH, 2])
        kb.dma("sp", modA[:], io["modA"])
        nmix = kb.sb("nmix", [128, NCH])
        kb.dma("sp", nmix[:], io["nmix"])
        kb.copy("dve", B1v[:], modA[:, 0, :, :])
        kb.ts("dve", A1v[:], modA[:, 1, :, :], 1.0, ALU.add)
        kb.tt("dve", A1v[:], A1v[:], nmix[:].unsqueeze(2).to_broadcast([128, NCH, 2]), ALU.mult)
        kb.close_scope()
    elif modsrc != "done":
        modsrc(A1v, B1v)
    if STOP == "A0":
        return
    kb.open_scope()
    wfb = kb.sb("wfb", [128, NCH, 1536], BF16)
    wf_v = W("wf").rearrange("(c p) n -> p c n", p=128)
    for i in range(3):
        kb.dma("pool", wfb[:, :, i * 512:(i + 1) * 512], wf_v[:, :, i * 512:(i + 1) * 512])
    wtb = kb.sb("wtb", [128, NCH, 264], BF16)
    kb.dma("pool", wtb[:], W("wt").rearrange("(c p) n -> p c n", p=128))
    xT_v = io["xT"].rearrange("(c p) t -> p c t", p=128)
    cT_v = io["ctxT"].rearrange("(c p) t -> p c t", p=128)
    groups = [("c", 0, L)] + [("l", g * 512, 512) for g in range(8)]
    evac_i = 0
    def a1_prep(kind, t0, n):
        w = 1 if kind == "c" else 0
        xt = kb.tmp("xt", [128, NCH, 512], F32, 2)
        src = cT_v if kind == "c" else xT_v[:, :, t0:t0 + n]
        kb.dma("sp", xt[:, :, 0:n], src)
        ssq = kb.bank()
        for c in range(NCH):
            sq = kb.tmp("sq", [128, 512], BF16, 3)
            kb.act(sq[:, 0:n], xt[:, c, 0:n], AF.Square)
            kb.mm(ssq[:, 0:n], onesb[:], sq[:, 0:n], start=(c == 0), stop=(c == NCH - 1))
        rn = kb.tmp("rn", [128, 512], F32, 2)
        kb.act(rn[:, 0:n], ssq[:, 0:n], AF.Sqrt, bias=EPS, scale=1.0 / D)
        kb.recip(rn[:, 0:n], rn[:, 0:n])
        hT = kb.tmp("hT", [128, NCH, 512], BF16, 2)
        for c in range(NCH):
            tf = kb.tmp("tf", [128, 512], F32, 3)
            kb.tt("dve", tf[:, 0:n], xt[:, c, 0:n], rn[:, 0:n], ALU.mult)
            kb.act(hT[:, c, 0:n], tf[:, 0:n], AF.Identity, bias=B1v[:, c, w:w + 1], scale=A1v[:, c, w:w + 1])
        return hT

    hT_next = a1_prep(*groups[0])
    for gi, (kind, t0, n) in enumerate(groups):
        hT = hT_next
        if gi + 1 < len(groups):
            hT_next = a1_prep(*groups[gi + 1])
        cbs = [0, 1, 2, 3, 8, 9] if kind == "c" else list(range(12))
        for cb in cbs:
            ps = kb.bank()
            for c in range(NCH):
                kb.mm(ps[:, 0:n], wfb[:, c, cb * 128:(cb + 1) * 128], hT[:, c, 0:n], start=(c == 0), stop=(c == NCH - 1))
            eng = "act" if evac_i % 2 == 0 else "dve"
            evac_i += 1
            if cb < 8:
                so = kb.tmp("so32", [128, 512], F32, 3)
                kb.copy(eng, so[:, 0:n], ps[:, 0:n])
                dst = fm32c[cb][:, :] if kind == "c" else fm32[cb][:, t0:t0 + n]
                kb.dma("pool", dst, so[:, 0:n])
            else:
                so = kb.tmp("sobf", [128, 512], BF16, 3)
                kb.copy(eng, so[:, 0:n], ps[:, 0:n])
                dst = fmbfc[cb - 8][:, :] if kind == "c" else fmbf[cb - 8][:, t0:t0 + n]
                kb.dma("pool", dst, so[:, 0:n])
        nsub = n // 128
        sg_ = kb.tmp("sogt", [128, 4, 8], F32, 2)
        for s in range(nsub):
            ps = kb.bank()
            for c in range(NCH):
                kb.mm(ps[:, 0:TMN], hT[:, c, s * 128:(s + 1) * 128], wtb[:, c, 0:TMN], start=(c == 0), stop=(c == NCH - 1))
            sv_ = kb.tmp("sovb", [128, 256], BF16, 3)
            kb.copy("act", sv_[:], ps[:, 0:256])
            kb.copy("act", sg_[:, s, :], ps[:, 256:264])
            ni = (t0 + s * 128) // 128
            if os.environ.get("SKIP_VBDMA"):
                pass
            elif kind == "c":
                kb.dma("pool", vbsc[:, ni, :], sv_[:])
            else:
                kb.dma("pool", vbs[:, ni, :], sv_[:])
        ni0 = t0 // 128
        if os.environ.get("SKIP_GTDMA"):
            pass
        elif kind == "c":
            kb.dma("pool", gtsc[:, 0:nsub, :], sg_[:, 0:nsub, :])
        else:
            kb.dma("pool", gts[:, ni0:ni0 + nsub, :], sg_[:, 0:nsub, :])
    kb.close_scope()

    if dbg is not None and "fm32" in dbg:
        kb.open_scope()
        for i in range(8):
            t_ = kb.tmp("dbgt", [128, T], F32, 1)
            kb.dma("sp", t_[:], fm32[i])
            kb.dma("sp", dbg["fm32"][i], t_[:])
        kb.close_scope()

    if STOP == "A1":
        return
    kb.open_scope()
    kbT = kb.sb("kbT", [128, 2, T], BF16)
    qbT = kb.sb("qbT", [128, 2, T], BF16)
    for hh in range(2):
        kb.dma("sp", kbT[:, hh, :], fmbf[hh])
        kb.dma("sp", qbT[:, hh, :], fmbf[2 + hh])
    vbt = kb.sb("vbt", [128, 32, 256], BF16)
    kb.dma("sp", vbt[:], vbs)
    kcx = kb.sb("kcx", [128, 2, L], BF16)
    for hh in range(2):
        kb.dma("sp", kcx[:, hh, :], fmbfc[hh])
    vcx = kb.sb("vcx", [128, 2, 256], BF16)
    kb.dma("sp", vcx[:], vbsc)
    biasT = kb.sb("biasT", [128, 5, 2, 640])
    for y in range(5):
        kb.dma("sp", biasT[:, y, :, :], W("nabias")[y].rearrange("h q k -> q h k"))
    ynaT = kb.sb("ynaT", [128, 2, T], BF16)
    for m in range(32):
        base = 128 * min(max(m - 2, 0), 27)
        typ = 0 if m == 0 else (1 if m == 1 else (2 if m <= 29 else (3 if m == 30 else 4)))
        for hh in range(2):
            q_ap = qbT[:, hh, 128 * m:128 * m + 128]
            psC = kb.bank()
            psA = kb.bank()
            psB = kb.bank()
            kb.mm(psC[:, 0:256], q_ap, kcx[:, hh, :])
            kb.mm(psA[:, 0:512], q_ap, kbT[:, hh, base:base + 512])
            kb.mm(psB[:, 0:128], q_ap, kbT[:, hh, base + 512:base + 640])
            Sb = kb.tmp("nasb", [128, 896], F32, 2)
            kb.act(Sb[:, 0:256], psC[:, 0:256], AF.Identity, scale=SCALE)
            kb.stt("dve", Sb[:, 256:768], psA[:, 0:512], SCALE, biasT[:, typ, hh, 0:512], ALU.mult, ALU.add)
            kb.stt("dve", Sb[:, 768:896], psB[:, 0:128], SCALE, biasT[:, typ, hh, 512:640], ALU.mult, ALU.add)
            mx = kb.tmp("namx", [128, 1], F32, 4)
            kb.rmax(mx[:], Sb[:])
            kb.ts("dve", mx[:], mx[:], -1.0, ALU.mult)
            P = kb.tmp("nap", [128, 896], BF16, 2)
            rs = kb.tmp("nars", [128, 1], F32, 4)
            kb.memset("dve", rs[:], 0.0)
            kb.act(P[:], Sb[:], AF.Exp, bias=mx[:], accum_out=rs[:])
            kb.recip(rs[:], rs[:])
            kb.ts("dve", P[:], P[:], rs[:], ALU.mult)
            pT = kb.bbank()
            pT2 = kb.bbank()
            for kc in range(4):
                kb.tr(pT[:, kc * 128:(kc + 1) * 128], P[:, kc * 128:(kc + 1) * 128], identb[:])
            for kc in range(4, 7):
                kb.tr(pT2[:, (kc - 4) * 128:(kc - 3) * 128], P[:, kc * 128:(kc + 1) * 128], identb[:])
            PT = kb.tmp("napt", [128, 896], BF16, 2)
            kb.copy("act", PT[:, 0:512], pT[:, 0:512])
            kb.copy("dve", PT[:, 512:896], pT2[:, 0:384])
            ops = kb.bank()
            for kc in range(7):
                if kc < 2:
                    v_ap = vcx[:, kc, hh * 128:(hh + 1) * 128]
                else:
                    v_ap = vbt[:, base // 128 + kc - 2, hh * 128:(hh + 1) * 128]
                kb.mm(ops[:, 0:128], v_ap, PT[:, kc * 128:(kc + 1) * 128], start=(kc == 0), stop=(kc == 6))
            kb.copy("act", ynaT[:, hh, 128 * m:128 * m + 128], ops[:, 0:128])
    for hh in range(2):
        kb.dma("sp", ymix[2 + hh], ynaT[:, hh, :])
    kb.close_scope()

    if STOP == "A2":
        return
    kb.open_scope()
    gp = kb.sb("gp", [128, 8])
    kb.dma("sp", gp[:], W("gpar"))
    nega = kb.sb("nega", [128, 4])
    kb.act(nega[:], gp[:, 0:4], AF.Exp)
    kb.ts("dve", nega[:], nega[:], -1.0, ALU.mult)
    convw = kb.sb("convw", [128, 3, 2, 5])
    kb.dma("sp", convw[:], W("convw"))
    gnorm = kb.sb("gnorm", [128, 1])
    kb.dma("sp", gnorm[:], io["gnorm"])

    def gate_prepass(gsrc, NT, tag):
        G, gc, egl = gate_core(kb, masks, gsrc, NT, tag, gp, nega)
        r = {}
        beta = kb.sb("beta" + tag, [128, 4, NT])
        kb.act(beta[:], G[:, :, 4:8].rearrange("p n c -> p c n"), AF.Sigmoid)
        lnb = kb.sb("lnb" + tag, [128, 4, NT])
        kb.act(lnb[:], beta[:], AF.Ln)
        glps = kb.bank()
        kb.mm(glps[:, 0:2 * NT], masks[:, M_SELF, :], gc[:, 0:2, :].rearrange("p c n -> p (c n)"))
        kb.mm(glps[:, 2 * NT:4 * NT], masks[:, M_SELB, :], gc[:, 2:4, :].rearrange("p c n -> p (c n)"))
        ekd = kb.sb("ekd" + tag, [128, 4, NT])
        kb.tt("dve", ekd[:].rearrange("p c n -> p (c n)"), glps[:, 0:4 * NT], gc[:].rearrange("p c n -> p (c n)"), ALU.subtract)
        kb.act(ekd[:], ekd[:], AF.Exp)
        egc = kb.sb("egc" + tag, [128, 4, NT])
        kb.act(egc[:], gc[:], AF.Exp)
        bk = kb.sb("bk" + tag, [128, 4, NT])
        kb.tt("dve", bk[:], beta[:], egc[:], ALU.mult)
        gcb = kb.sb("gcb" + tag, [128, 4, NT])
        kb.tt("dve", gcb[:], gc[:], lnb[:], ALU.add)
        r.update(beta=beta, gc=gc, gcb=gcb, bk=bk, ekd=ekd, egl=egl)
        return r

    gl = gate_prepass(gts, 32, "l")
    gx = gate_prepass(gtsc, 2, "c")
    if dbg is not None and "gc" in dbg:
        kb.dma("sp", dbg["gc"], gl["gc"][:])
        kb.dma("sp", dbg["beta"], gl["beta"][:])

    def neumann_multi(MTs, NB, res):
        W = NB * 128
        st = []
        for MTb in MTs:
            ps = kb.bank()
            for i in range(NB):
                kb.tr(ps[:, i * 128:(i + 1) * 128], MTb[:, i, :].bitcast(F32), ident[:])
            X = kb.tmp("nmX", [128, 4, 128], R32, 6)
            kb.copy("act", X[:, 0:NB, :].rearrange("p a b -> p (a b)"), ps[:, 0:W])
            TT = kb.tmp("nmT", [128, 4, 128], R32, 6)
            kb.tt("dve", TT[:, 0:NB, :], ident[:].unsqueeze(1).to_broadcast([128, NB, 128]), MTb[:, 0:NB, :], ALU.subtract)
            st.append([X, MTb, TT])
            yield
        for k in range(1, 6):
            for e in st:
                X, XT, TT = e
                ps = kb.bank()
                for i in range(NB):
                    kb.mm(ps[:, i * 128:(i + 1) * 128], XT[:, i, :], X[:, i, :])
                if k < 5:
                    psx = kb.bank()
                    for i in range(NB):
                        kb.mm(psx[:, i * 128:(i + 1) * 128], X[:, i, :], XT[:, i, :])
                Xn = kb.tmp("nmX", [128, 4, 128], R32, 6)
                kb.copy("act", Xn[:, 0:NB, :].rearrange("p a b -> p (a b)"), ps[:, 0:W])
                if k < 5:
                    XTn = kb.tmp("nmXT", [128, 4, 128], R32, 6)
                    kb.copy("dve", XTn[:, 0:NB, :].rearrange("p a b -> p (a b)"), psx[:, 0:W])
                ps2 = kb.bank()
                for i in range(NB):
                    kb.mm(ps2[:, i * 128:(i + 1) * 128], Xn[:, i, :], TT[:, i, :])
                TTn = kb.tmp("nmT", [128, 4, 128], R32, 6)
                kb.tt("dve", TTn[:, 0:NB, :].rearrange("p a b -> p (a b)"),
                      TT[:, 0:NB, :].rearrange("p a b -> p (a b)"), ps2[:, 0:W], ALU.add)
                e[0] = Xn
                e[2] = TTn
                if k < 5:
                    e[1] = XTn
                yield
        res["TTs"] = [e[2] for e in st]

    def run_interleaved(gens):
        gens = [g_ for g_ in gens if g_ is not None]
        while gens:
            for g_ in list(gens):
                try:
                    next(g_)
                except StopIteration:
                    gens.remove(g_)

    def gdn_pre(kind):
        ctx_mode = (kind == "c")
        gd = gx if ctx_mode else gl
        grp_list = [(0, L)] if ctx_mode else [(g * 512, 512) for g in range(8)]
        src = fm32c if ctx_mode else fm32
        TL = L if ctx_mode else T
        sp = spillc if ctx_mode else spill
        segs = [0, 1] if ctx_mode else [0, 1, 2]
        cs_hold = {}

        def P(t0, n, hh, cur):
            if not ctx_mode and hh == 0:
                cs = kb.tmp("cs", [128, 2, 512], F32, 2)
                kb.dma("sp", cs[:, 0, :], io["cos"][:, t0:t0 + n])
                kb.dma("sp", cs[:, 1, :], io["sin"][:, t0:t0 + n])
                cs_hold[t0] = cs
            cs = cs_hold.get(t0)
            outs = []
            for seg in segs:
                raw = kb.tmp("raw", [128, 516], F32, 3)
                lo = max(t0 - 2, 0)
                hi = min(t0 + n + 2, TL)
                d0 = lo - (t0 - 2)
                if d0 > 0:
                    kb.memset("dve", raw[:, 0:d0], 0.0)
                if hi < t0 + n + 2:
                    kb.memset("dve", raw[:, n + 4 - (t0 + n + 2 - hi):n + 4], 0.0)
                kb.dma("sp", raw[:, d0:d0 + (hi - lo)], src[seg * 2 + hh][:, lo:hi])
                cv_ = kb.tmp("cv%d" % seg, [128, 512], F32, 2)
                kb.ts("dve", cv_[:, 0:n], raw[:, 0:n], convw[:, seg, hh, 0:1], ALU.mult)
                for k in range(1, 5):
                    kb.stt("dve", cv_[:, 0:n], raw[:, k:k + n], convw[:, seg, hh, k:k + 1], cv_[:, 0:n],
                           ALU.mult, ALU.add)
                kb.act(cv_[:, 0:n], cv_[:, 0:n], AF.Silu)
                outs.append(cv_)
                yield
            kc_, vc_ = outs[0], outs[1]
            qc_ = outs[2] if not ctx_mode else None
            fin = []
            for which, xc in (("k", kc_), ("q", qc_)):
                if xc is None:
                    fin.append(None)
                    continue
                sq = kb.tmp("gsq", [128, 512], R32, 2)
                kb.act(sq[:, 0:n], xc[:, 0:n], AF.Square)
                nps = kb.bank()
                kb.mm(nps[:, 0:n], onesr[:], sq[:, 0:n])
                rn = kb.tmp("grn", [128, 512], F32, 2)
                kb.act(rn[:, 0:n], nps[:, 0:n], AF.Sqrt, bias=EPS)
                kb.recip(rn[:, 0:n], rn[:, 0:n])
                xn = kb.tmp("gxn" + which, [128, 512], R32, 2)
                if which == "q":
                    kb.stt("dve", xn[:, 0:n], xc[:, 0:n], SCALE, rn[:, 0:n], ALU.mult, ALU.mult)
                else:
                    kb.tt("dve", xn[:, 0:n], xc[:, 0:n], rn[:, 0:n], ALU.mult)
                if ctx_mode:
                    fin.append(xn)
                    continue
                rps = kb.bank()
                kb.mm(rps[:, 0:n], masksr[:, M_ROT, :], xn[:, 0:n])
                t1 = kb.tmp("grt1", [128, 512], F32, 2)
                kb.tt("dve", t1[:, 0:n], xn[:, 0:n], cs[:, 0, 0:n], ALU.mult)
                t2 = kb.tmp("grt2", [128, 512], F32, 2)
                kb.tt("dve", t2[:, 0:n], rps[:, 0:n], cs[:, 1, 0:n], ALU.mult)
                xr = kb.tmp("gxr" + which, [128, 512], R32, 2)
                kb.tt("dve", xr[:, 0:n], t1[:, 0:n], t2[:, 0:n], ALU.add)
                fin.append(xr)
                yield
            KT, QT = fin
            if dbg is not None and "KT" in dbg and not ctx_mode and t0 == 0:
                kb.dma("sp", dbg["KT"][hh], KT[:])
                kb.dma("sp", dbg["QT"][hh], QT[:])
                kb.dma("sp", dbg["VT"][hh], vc_[:])
            NB = n // 128
            tn0 = t0 // 128
            MTb = [kb.tmp("gMTb%d" % d, [128, 4, 128], R32, 2) for d in range(2)]
            kbgb = [kb.tmp("gkbgb%d" % d, [128, 4, 128], R32, 2) for d in range(2)]
            vbeb = [kb.tmp("gvbeb%d" % d, [128, 4, 128], R32, 2) for d in range(2)]
            stgb = [kb.tmp("gstgb%d" % d, [128, 4, 5, 128], F32, 2) for d in range(2)]
            Wd = NB * 128
            v3 = lambda ap: ap.rearrange("p (a b) -> p a b", b=128)
            tpK = kb.bank()
            tpV = kb.bank()
            for s in range(NB):
                sl = slice(s * 128, (s + 1) * 128)
                kb.tr(tpK[:, sl], KT[:, sl].bitcast(F32), ident[:])
                kb.tr(tpV[:, sl], vc_[:, sl], ident[:])
            for d in range(2):
                c4 = d * 2 + hh
                sc = lambda nm: gd[nm][:, c4, tn0:tn0 + NB].unsqueeze(2).to_broadcast([128, NB, 128])
                kb.tt("dve", kbgb[d][:, 0:NB, :], v3(tpK[:, 0:Wd]), sc("bk"), ALU.mult)
                kb.tt("dve", stgb[d][:, 0:NB, 4, :], v3(tpK[:, 0:Wd]), sc("ekd"), ALU.mult)
                kb.tt("dve", vbeb[d][:, 0:NB, :], v3(tpV[:, 0:Wd]), sc("beta"), ALU.mult)
            yield
            kkK = kb.bank()
            for s in range(NB):
                sl = slice(s * 128, (s + 1) * 128)
                kb.mm(kkK[:, sl], KT[:, sl], KT[:, sl])
            if not ctx_mode:
                kkQ = kb.bank()
                for s in range(NB):
                    sl = slice(s * 128, (s + 1) * 128)
                    kb.mm(kkQ[:, sl], KT[:, sl], QT[:, sl])
            for d in range(2):
                c4 = d * 2 + hh
                sc = lambda nm: gd[nm][:, c4, tn0:tn0 + NB].unsqueeze(2).to_broadcast([128, NB, 128])
                for kind in ((1,) if ctx_mode else (0, 1)):
                    dgt = kb.tmp("gdg", [128, 4, 128], R32, 2)
                    kb.tt("dve", dgt[:, 0:NB, :], ident[:].unsqueeze(1).to_broadcast([128, NB, 128]),
                          sc("gc" if kind == 0 else "gcb"), ALU.mult)
                    bc = kb.bank()
                    kb.mm(bc[:, 0:Wd], onesr[:], dgt[:, 0:NB, :].rearrange("p a b -> p (a b)"))
                    Lt = kb.tmp("gLt", [128, 4, 128], F32, 2)
                    kb.tt("dve", Lt[:, 0:NB, :], v3(bc[:, 0:Wd]), sc("gc"), ALU.subtract)
                    if kind == 0:
                        mi = M_AIF if d == 0 else M_AIB
                    else:
                        mi = M_ASF if d == 0 else M_ASB
                    kb.tt("dve", Lt[:, 0:NB, :], Lt[:, 0:NB, :], masks[:, mi, :].unsqueeze(1).to_broadcast([128, NB, 128]), ALU.add)
                    DDt = kb.tmp("gDD", [128, 4, 128], F32, 2)
                    kb.act(DDt[:, 0:NB, :], Lt[:, 0:NB, :], AF.Exp)
                    if kind == 1:
                        kb.tt("dve", MTb[d][:, 0:NB, :], v3(kkK[:, 0:Wd]), DDt[:, 0:NB, :], ALU.mult)
                    else:
                        ebc = kb.tmp("gebc", [128, 4, 128], F32, 2)
                        kb.act(ebc[:, 0:NB, :], v3(bc[:, 0:Wd]), AF.Exp)
                        kb.tt("dve", stgb[d][:, 0:NB, 2, :], v3(kkQ[:, 0:Wd]), DDt[:, 0:NB, :], ALU.mult)
                        kb.tt("dve", stgb[d][:, 0:NB, 1, :], v3(QT[:, 0:Wd]), ebc[:, 0:NB, :], ALU.mult)
            yield

            cur.update(MTb=MTb, kbgb=kbgb, vbeb=vbeb, stgb=stgb, NB=NB, tn0=tn0, hh=hh)

        def N(cur):
            MTb, kbgb, vbeb, stgb, NB, tn0, hh = (cur[k_] for k_ in ('MTb', 'kbgb', 'vbeb', 'stgb', 'NB', 'tn0', 'hh'))
            res = {}
            yield from neumann_multi(MTb, NB, res)
            TTs = res['TTs']
            for d in range(2):
                c4 = d * 2 + hh
                ups = kb.bank()
                wps = kb.bank()
                for s in range(NB):
                    kb.mm(ups[:, s * 128:(s + 1) * 128], TTs[d][:, s, :], vbeb[d][:, s, :])
                    kb.mm(wps[:, s * 128:(s + 1) * 128], kbgb[d][:, s, :], TTs[d][:, s, :])
                kb.copy("act", stgb[d][:, 0:NB, 3, :], ups[:, 0:NB * 128].rearrange("p (a b) -> p a b", b=128))
                kb.copy("dve", stgb[d][:, 0:NB, 0, :], wps[:, 0:NB * 128].rearrange("p (a b) -> p a b", b=128))
                for s in range(NB):
                    if ctx_mode:
                        kb.dma("pool", sp[c4, tn0 + s, 0].unsqueeze(1), stgb[d][:, s, 0:1, :])
                        kb.dma("pool", sp[c4, tn0 + s, 3:5].rearrange("o t k -> t o k"), stgb[d][:, s, 3:5, :])
                    else:
                        kb.dma("pool", sp[c4, tn0 + s].rearrange("o t k -> t o k"), stgb[d][:, s, :, :])
                yield


        prev = None
        for (t0, n) in grp_list:
            for hh in range(2):
                cur = {}
                run_interleaved([N(prev) if prev is not None else None, P(t0, n, hh, cur)])
                prev = cur
        run_interleaved([N(prev)])

    gdn_pre("c")
    gdn_pre("l")
    kb.close_scope()

    if dbg is not None and "spill" in dbg:
        kb.open_scope()
        for c4 in range(4):
            t_ = kb.tmp("dbgs", [128, 5, 128], F32, 2)
            kb.dma("sp", t_[:], spill[c4, 0].rearrange("o t k -> t o k"))
            kb.dma("sp", dbg["spill"][c4].rearrange("o t k -> t o k"), t_[:])
        kb.close_scope()

    if STOP == "A3":
        return
    if do_scan:
        phase_a_scan(nc, kb, io, sh, [(gidx, g)], [ymix], dbg=dbg)


def phase_a_scan(nc, kb, io, sh, glist, ymix_dsts, dbg=None):
    ident, masks, onesr = sh["ident"], sh["masks"], sh["onesr"]
    NGR = len(glist)
    kb.open_scope()
    gnorm = kb.sb("gnorm2", [128, 1])
    kb.dma("sp", gnorm[:], io["gnorm"])
    egl_l, egl_c, oacc, S = {}, {}, {}, {}
    for (gi, g) in glist:
        gp = kb.sb("gp2_%d" % gi, [128, 8])
        kb.dma("sp", gp[:], io["gpar"] if g is None else io["gpar"][g])
        nega = kb.sb("nega2_%d" % gi, [128, 4])
        kb.act(nega[:], gp[:, 0:4], AF.Exp)
        kb.ts("dve", nega[:], nega[:], -1.0, ALU.mult)
        egl_l[gi] = gate_core(kb, masks, sh["gts"][gi], 32, "ls%d" % gi, gp, nega)[2]
        egl_c[gi] = gate_core(kb, masks, sh["gtsc"][gi], 2, "cs%d" % gi, gp, nega)[2]
        oacc[gi] = kb.sb("oacc%d" % gi, [128, 2, T])
        for c4 in range(4):
            s0 = kb.sb("S%d_%da" % (gi, c4), [128, 128], R32)
            s1 = kb.sb("S%d_%db" % (gi, c4), [128, 128], R32)
            kb.ts("dve", s0[:], ident[:], 0.0, ALU.mult)
            S[(gi, c4)] = [s0, s1, 0]
    visited = set()

    def scan_step(kind, items):
        ctx_mode = (kind == "c")
        ld = {}
        for (gi, c4, tn) in items:
            ck = (gi, c4)
            sp = sh["spillc"][gi] if ctx_mode else sh["spill"][gi]
            Fm = kb.tmp("scF%d_%d" % ck, [128, 2, 128], F32, 1)
            if ctx_mode:
                kb.dma("sp", Fm[:, 0:1, :], sp[c4, tn, 0].unsqueeze(1))
            else:
                kb.dma("sp", Fm[:], sp[c4, tn, 0:2].rearrange("o t k -> t o k"))
            Tk = kb.tmp("scT%d_%d" % ck, [64, 2, 3, 128], F32, 1)
            for q in range(2):
                if ctx_mode:
                    kb.dma("sp", Tk[:, q, 1:3, :], sp[c4, tn, 3:5, q * 64:(q + 1) * 64, :].rearrange("o j k -> j o k"))
                else:
                    kb.dma("sp", Tk[:, q, :, :], sp[c4, tn, 2:5, q * 64:(q + 1) * 64, :].rearrange("o j k -> j o k"))
            Fr = kb.tmp("scFr%d_%d" % ck, [128, 2, 128], R32, 2)
            Tr = kb.tmp("scTr%d_%d" % ck, [64, 2, 2, 128], R32, 2)
            if ctx_mode:
                kb.copy("act", Fr[:, 0:1, :], Fm[:, 0:1, :])
                kb.copy("pool", Tr[:, :, 1, :], Tk[:, :, 2, :])
            else:
                kb.copy("act", Fr[:], Fm[:])
                kb.copy("pool", Tr[:, :, 0:2, :], Tk[:, :, 0:3:2, :])
            ld[ck] = (Fr, Tk, Tr)
        for qi in range(2):
            work = []
            for (gi, c4, tn) in items:
                ck = (gi, c4)
                d, hh = c4 // 2, c4 % 2
                q = qi if d == 0 else 1 - qi
                Fr, Tk, Tr = ld[ck]
                st = S[ck]
                cur, nxt = st[st[2]], st[1 - st[2]]
                st[2] = 1 - st[2]
                work.append((ck, gi, c4, tn, d, hh, q, Fr, Tk, Tr, cur, nxt))
            vpss, vn, spss, opss = {}, {}, {}, {}
            for b0 in range(0, len(work), 3):
                sub = work[b0:b0 + 3]
                for (ck, gi, c4, tn, d, hh, q, Fr, Tk, Tr, cur, nxt) in sub:
                    vps = kb.bank()
                    kb.mm(vps[0:64, 0:128], Fr[:, 0, q * 64:(q + 1) * 64], cur[:])
                    vpss[ck] = vps
                for (ck, gi, c4, tn, d, hh, q, Fr, Tk, Tr, cur, nxt) in sub:
                    vnew = kb.tmp("scv%d_%d" % ck, [64, 128], R32, 2)
                    kb.tt("dve", vnew[:], Tk[:, q, 1, :], vpss[ck][0:64, 0:128], ALU.subtract)
                    vn[ck] = vnew
                for (ck, gi, c4, tn, d, hh, q, Fr, Tk, Tr, cur, nxt) in sub:
                    sps = kb.bank()
                    kb.mm(sps[:, 0:128], Tr[:, q, 1, :], vn[ck][:])
                    spss[ck] = sps
                for (ck, gi, c4, tn, d, hh, q, Fr, Tk, Tr, cur, nxt) in sub:
                    kb.stt("dve", nxt[:], cur[:], (egl_c if ctx_mode else egl_l)[gi][:, q, c4, tn:tn + 1],
                           spss[ck][:, 0:128], ALU.mult, ALU.add)
                if ctx_mode:
                    continue
                for (ck, gi, c4, tn, d, hh, q, Fr, Tk, Tr, cur, nxt) in sub:
                    ops_ = kb.bank()
                    qs = slice(q * 64, (q + 1) * 64)
                    kb.mm(ops_[:, 0:64], cur[:], Fr[:, 1, qs], start=True, stop=False)
                    kb.mm(ops_[:, 0:64], vn[ck][:], Tr[:, q, 0, qs], start=False, stop=True)
                    opss[ck] = ops_
                for (ck, gi, c4, tn, d, hh, q, Fr, Tk, Tr, cur, nxt) in sub:
                    tok0 = tn * 128 + q * 64
                    key = (gi, hh, tok0)
                    if key not in visited:
                        visited.add(key)
                        kb.copy("act", oacc[gi][:, hh, tok0:tok0 + 64], opss[ck][:, 0:64])
                    else:
                        kb.tt("dve", oacc[gi][:, hh, tok0:tok0 + 64], oacc[gi][:, hh, tok0:tok0 + 64],
                              opss[ck][:, 0:64], ALU.add)

    for step in range(2):
        scan_step("c", [(gi, c4, step if c4 // 2 == 0 else 1 - step) for (gi, g) in glist for c4 in range(4)])
    for step in range(32):
        scan_step("l", [(gi, c4, step if c4 // 2 == 0 else 31 - step) for (gi, g) in glist for c4 in range(4)])

    if dbg is not None and "oacc" in dbg:
        kb.dma("sp", dbg["oacc"], oacc[glist[0][0]][:])

    for k, (gi, g) in enumerate(glist):
        for hh in range(2):
            for tg in range(8):
                sl = slice(tg * 512, (tg + 1) * 512)
                z = kb.tmp("oz", [128, 512], F32, 1)
                kb.dma("sp", z[:], sh["fm32"][gi][6 + hh][:, sl])
                kb.act(z[:], z[:], AF.Silu)
                sq = kb.tmp("osq", [128, 512], R32, 1)
                kb.act(sq[:], oacc[gi][:, hh, sl], AF.Square)
                nps = kb.bank()
                kb.mm(nps[:], onesr[:], sq[:])
                rn = kb.tmp("orn", [128, 512], F32, 1)
                kb.act(rn[:], nps[:], AF.Sqrt, bias=EPS, scale=1.0 / 128)
                kb.recip(rn[:], rn[:])
                t_ = kb.tmp("ot", [128, 512], F32, 1)
                kb.tt("dve", t_[:], oacc[gi][:, hh, sl], rn[:], ALU.mult)
                yo = kb.tmp("oy", [128, 512], BF16, 2)
                kb.stt("dve", yo[:], z[:], gnorm[:, 0:1], t_[:], ALU.mult, ALU.mult)
                kb.dma("sp", ymix_dsts[k][hh][:, sl], yo[:])
    kb.close_scope()


D = 2048
NCH = 16
FH = 5632
NT = 1026
EPS = 1e-6
TILES_ALL = [(0, 512), (512, 512), (1024, 2)]
TILES_MAIN = [(1, 512), (513, 512)]


def ada_io(nc):
    io = {}
    io["svec"] = nc.dram_tensor("svec", [128, NCH, 3], F32, kind="ExternalInput").ap()
    io["adaw"] = nc.dram_tensor("adaw", [D, 3072], F32, kind="ExternalInput").ap()
    io["adab"] = nc.dram_tensor("adab", [1, 3072], F32, kind="ExternalInput").ap()
    io["mod"] = nc.dram_tensor("mod", [3, 3072], F32, kind="ExternalOutput").ap()
    return io


def ada_prog(nc, kb, io):
    cv = kb.sb("cv", [128, NCH, 3])
    kb.dma("sp", cv[:], io["svec"])
    sv = kb.sb("sv", [128, NCH, 3])
    kb.act(sv[:], cv[:], AF.Silu)
    bb = kb.sb("bb", [3, 3072])
    for r in range(3):
        kb.dma("sp", bb[r:r + 1, :], io["adab"])
    adaw_v = io["adaw"].rearrange("(c p) n -> p c n", p=128)
    res = kb.sb("res", [3, 3072])
    for blk in range(6):
        wa = kb.tmp("adaw", [128, NCH, 512], F32, 2)
        kb.dma("sp", wa[:], adaw_v[:, :, blk * 512:(blk + 1) * 512])
        ps = kb.bank()
        for c in range(NCH):
            kb.mm(ps[0:3, :], sv[:, c, :], wa[:, c, :], start=(c == 0), stop=(c == NCH - 1))
        kb.tt("dve", res[:, blk * 512:(blk + 1) * 512], ps[0:3, :], bb[:, blk * 512:(blk + 1) * 512], ALU.add)
    kb.dma("sp", io["mod"], res[:])


def phase_b_io(nc):
    io = {}

    def I(name, shape, dt=F32):
        io[name] = nc.dram_tensor(name, list(shape), dt, kind="ExternalInput").ap()

    I("xT", [D, NT]); I("ymixT", [D, NT], BF16); I("hmask", [128, 2]); I("modB", [128, 10, NCH])
    I("norms", [128, 4, NCH])
    I("w_out0", [D, D]); I("wg0", [D, FH]); I("wu0", [D, FH]); I("wd0", [FH, D])
    I("w_in1", [D, 3 * D]); I("conv1", [128, NCH, 3]); I("w_out1", [D, D])
    I("wg1", [D, FH]); I("wu1", [D, FH]); I("wd1", [FH, D])
    io["outT"] = nc.dram_tensor("outT", [D, 1024], F32, kind="ExternalOutput").ap()
    return io


def phase_b(nc, kb, io, fill=None):
    onesb = kb.sb("onesb", [128, 128], BF16)
    kb.memset("dve", onesb[:], 1.0)
    xT = kb.sb("xT", [128, NCH, NT])
    kb.dma("sp", xT[:], (io["xTo"] if fill is not None else io["xT"]).rearrange("(c p) t -> p c t", p=128))
    bufA = kb.sb("bufA", [128, NCH, NT], BF16)
    bufB = kb.sb("bufB", [128, NCH, NT], BF16)
    modB = kb.sb("modB", [128, 10, NCH])
    if fill is None:
        kb.dma("sp", bufA[:], io["ymixT"].rearrange("(c p) t -> p c t", p=128))
        kb.dma("sp", modB[:], io["modB"])
    else:
        fill(bufA, modB)
    norms = kb.sb("norms", [128, 4, NCH])
    kb.dma("sp", norms[:], io["norms"])
    hmask = kb.sb("hmask", [128, 2])
    kb.dma("sp", hmask[:], io["hmask"])
    conv1 = kb.sb("conv1", [128, NCH, 3])
    kb.dma("sp", conv1[:], io["conv1"])
    coefA = kb.sb("coefA", [128, 4, NCH])
    for i, (ni, sci) in enumerate(((0, 2), (1, 5), (2, 8))):
        kb.ts("dve", coefA[:, i, :], modB[:, sci, :], 1.0, ALU.add)
        kb.tt("dve", coefA[:, i, :], coefA[:, i, :], norms[:, ni, :], ALU.mult)

    def modulate(dst, tiles, Acoef, Bcoef):
        for (c0, n) in tiles:
            ssq = kb.bank()
            for c in range(NCH):
                sq = kb.tmp("sq", [128, 512], BF16, 2)
                kb.act(sq[:, 0:n], xT[:, c, c0:c0 + n], AF.Square)
                kb.mm(ssq[:, 0:n], onesb[:], sq[:, 0:n], start=(c == 0), stop=(c == NCH - 1))
            rn = kb.tmp("rn", [128, 512], F32, 1)
            kb.act(rn[:, 0:n], ssq[:, 0:n], AF.Sqrt, bias=EPS, scale=1.0 / D)
            kb.recip(rn[:, 0:n], rn[:, 0:n])
            for c in range(NCH):
                tf = kb.tmp("tf", [128, 512], F32, 2)
                kb.tt("dve", tf[:, 0:n], xT[:, c, c0:c0 + n], rn[:, 0:n], ALU.mult)
                kb.act(dst[:, c, c0:c0 + n], tf[:, 0:n], AF.Identity, bias=Bcoef[:, c:c + 1], scale=Acoef[:, c:c + 1])

    def out_proj(W, src, tiles, gain):
        wv = W.rearrange("(c p) n -> p c n", p=128)
        for grp in range(8):
            wt = kb.tmp("wo", [128, NCH, 256], BF16, 2)
            kb.dma("pool", wt[:], wv[:, :, grp * 256:(grp + 1) * 256])
            for j in range(2):
                m = grp * 2 + j
                for (c0, n) in tiles:
                    ps = kb.bank()
                    for c in range(NCH):
                        kb.mm(ps[:, 0:n], wt[:, c, j * 128:(j + 1) * 128], src[:, c, c0:c0 + n],
                              start=(c == 0), stop=(c == NCH - 1))
                    kb.stt("dve", xT[:, m, c0:c0 + n], ps[:, 0:n], gain[:, m:m + 1], xT[:, m, c0:c0 + n],
                           ALU.mult, ALU.add)

    def ffn(Wg, Wu, Wd, hsrc, tiles, gain):
        wgv = Wg.rearrange("(c p) n -> p c n", p=128)
        wuv = Wu.rearrange("(c p) n -> p c n", p=128)
        for grp in range(FH // 256):
            wg = kb.tmp("wg", [128, NCH, 256], BF16, 2)
            kb.dma("pool", wg[:], wgv[:, :, grp * 256:(grp + 1) * 256])
            wu = kb.tmp("wu", [128, NCH, 256], BF16, 2)
            kb.dma("pool", wu[:], wuv[:, :, grp * 256:(grp + 1) * 256])
            wd = kb.tmp("wd", [128, 2, D], BF16, 2)
            kb.dma("pool", wd[:], Wd[grp * 256:(grp + 1) * 256, :].rearrange("(j p) n -> p j n", p=128))
            hid = kb.tmp("hid", [128, 2, NT], BF16, 2)
            for j in range(2):
                for (c0, n) in tiles:
                    psg = kb.bank()
                    for c in range(NCH):
                        kb.mm(psg[:, 0:n], wg[:, c, j * 128:(j + 1) * 128], hsrc[:, c, c0:c0 + n],
                              start=(c == 0), stop=(c == NCH - 1))
                    psu = kb.bank()
                    for c in range(NCH):
                        kb.mm(psu[:, 0:n], wu[:, c, j * 128:(j + 1) * 128], hsrc[:, c, c0:c0 + n],
                              start=(c == 0), stop=(c == NCH - 1))
                    sg = kb.tmp("sg", [128, 512], F32, 2)
                    kb.act(sg[:, 0:n], psg[:, 0:n], AF.Silu)
                    kb.tt("dve", hid[:, j, c0:c0 + n], sg[:, 0:n], psu[:, 0:n], ALU.mult)
            for m in range(NCH):
                for (c0, n) in tiles:
                    ps = kb.bank()
                    for j in range(2):
                        kb.mm(ps[:, 0:n], wd[:, j, m * 128:(m + 1) * 128], hid[:, j, c0:c0 + n],
                              start=(j == 0), stop=(j == 1))
                    kb.stt("dve", xT[:, m, c0:c0 + n], ps[:, 0:n], gain[:, m:m + 1], xT[:, m, c0:c0 + n],
                           ALU.mult, ALU.add)

    kb.open_scope()
    out_proj(io["w_out0"], bufA, TILES_ALL, modB[:, 0, :])
    kb.close_scope()
    kb.open_scope()
    modulate(bufB, TILES_ALL, coefA[:, 0, :], modB[:, 1, :])
    ffn(io["wg0"], io["wu0"], io["wd0"], bufB, TILES_ALL, modB[:, 3, :])
    kb.close_scope()

    kb.open_scope()
    modulate(bufA, TILES_ALL, coefA[:, 1, :], modB[:, 4, :])
    w1v = io["w_in1"].rearrange("(c p) n -> p c n", p=128)
    for grp in range(8):
        ws = []
        for seg in range(3):
            wt = kb.tmp("wi%d" % seg, [128, NCH, 256], BF16, 2)
            kb.dma("pool", wt[:], w1v[:, :, seg * D + grp * 256: seg * D + (grp + 1) * 256])
            ws.append(wt)
        for j in range(2):
            m = grp * 2 + j
            u = kb.tmp("u", [128, NT], F32, 1)
            gbs = kb.tmp("gbs", [128, NT], F32, 1)
            for (c0, n) in TILES_ALL:
                pss = []
                for seg in range(3):
                    ps = kb.bank()
                    for c in range(NCH):
                        kb.mm(ps[:, 0:n], ws[seg][:, c, j * 128:(j + 1) * 128], bufA[:, c, c0:c0 + n],
                              start=(c == 0), stop=(c == NCH - 1))
                    pss.append(ps)
                gcs = kb.tmp("gcs", [128, 512], F32, 2)
                kb.copy("act", gcs[:, 0:n], pss[1][:, 0:n])
                kb.tt("dve", u[:, c0:c0 + n], gcs[:, 0:n], pss[2][:, 0:n], ALU.mult)
                kb.copy("act", gbs[:, c0:c0 + n], pss[0][:, 0:n])
            kb.tt("dve", u[:, 0:1], u[:, 0:1], hmask[:, 0:1], ALU.mult)
            kb.tt("dve", u[:, NT - 1:NT], u[:, NT - 1:NT], hmask[:, 1:2], ALU.mult)
            cvo = kb.tmp("cvo", [128, 1024], F32, 1)
            kb.ts("dve", cvo[:], u[:, 0:1024], conv1[:, m, 0:1], ALU.mult)
            kb.stt("dve", cvo[:], u[:, 1:1025], conv1[:, m, 1:2], cvo[:], ALU.mult, ALU.add)
            kb.stt("dve", cvo[:], u[:, 2:1026], conv1[:, m, 2:3], cvo[:], ALU.mult, ALU.add)
            kb.tt("dve", bufB[:, m, 1:1025], cvo[:], gbs[:, 1:1025], ALU.mult)
    kb.close_scope()
    kb.open_scope()
    out_proj(io["w_out1"], bufB, TILES_MAIN, modB[:, 6, :])
    kb.close_scope()

    kb.open_scope()
    modulate(bufA, TILES_MAIN, coefA[:, 2, :], modB[:, 7, :])
    ffn(io["wg1"], io["wu1"], io["wd1"], bufA, TILES_MAIN, modB[:, 9, :])
    kb.close_scope()

    kb.open_scope()
    for (c0, n) in TILES_MAIN:
        ssq = kb.bank()
        for c in range(NCH):
            sq = kb.tmp("sq", [128, 512], BF16, 2)
            kb.act(sq[:, 0:n], xT[:, c, c0:c0 + n], AF.Square)
            kb.mm(ssq[:, 0:n], onesb[:], sq[:, 0:n], start=(c == 0), stop=(c == NCH - 1))
        rn = kb.tmp("rn", [128, 512], F32, 1)
        kb.act(rn[:, 0:n], ssq[:, 0:n], AF.Sqrt, bias=EPS, scale=1.0 / D)
        kb.recip(rn[:, 0:n], rn[:, 0:n])
        for c in range(NCH):
            of = kb.tmp("of", [128, 512], F32, 3)
            kb.stt("dve", of[:, 0:n], xT[:, c, c0:c0 + n], norms[:, 3, c:c + 1], rn[:, 0:n], ALU.mult, ALU.mult)
            kb.dma("sp", io["outT"][c * 128:(c + 1) * 128, c0 - 1:c0 - 1 + n], of[:, 0:n])
    kb.close_scope()


def fused_io(nc):
    io = {}

    def I(name, shape, dt=F32):
        io[name] = nc.dram_tensor(name, list(shape), dt, kind="ExternalInput").ap()

    I("xT", [D, T]); I("ctxT", [D, L]); I("nmix", [128, NCH])
    I("wf", [4, D, 1536]); I("wt", [4, D, 264]); I("convw", [4, 128, 3, 2, 5]); I("gpar", [4, 128, 8])
    I("gnorm", [128, 1]); I("nabias", [4, 5, 2, 128, 640]); I("ident", [128, 128]); I("cmask", [128, NMASK, 128])
    I("cos", [128, T]); I("sin", [128, T])
    I("svec", [128, NCH, 2]); I("adaw", [2, D, 12288]); I("adab", [2, 12288]); I("i2", [2, 2])
    I("xTo", [D, NT]); I("hmask", [128, 2]); I("jsel", [128, 4]); I("norms", [128, 4, NCH])
    I("w_out0", [D, D]); I("wg0", [D, FH]); I("wu0", [D, FH]); I("wd0", [FH, D])
    I("w_in1", [D, 3 * D]); I("conv1", [128, NCH, 3]); I("w_out1", [D, D])
    I("wg1", [D, FH]); I("wu1", [D, FH]); I("wd1", [FH, D])
    io["outT"] = nc.dram_tensor("outT", [D, 1024], F32, kind="ExternalOutput").ap()
    return io


def fused_prog(nc, kb, io):
    modG = kb.sb("modG", [128, 192, 2])
    kb.open_scope()
    cv = kb.sb("cv", [128, NCH, 2])
    kb.dma("sp", cv[:], io["svec"])
    svb = kb.sb("svb", [128, NCH, 2], BF16)
    kb.act(svb[:], cv[:], AF.Silu)
    i2 = kb.sb("i2", [2, 2])
    kb.dma("sp", i2[:], io["i2"])
    rows = kb.sb("rows", [2, 2 * 12288])
    for l in range(2):
        for r in range(2):
            kb.dma("sp", rows[r:r + 1, l * 12288:(l + 1) * 12288], io["adab"][l:l + 1, :])
    for l in range(2):
        wv = io["adaw"][l].rearrange("(c p) n -> p c n", p=128)
        for blk in range(24):
            wa = kb.tmp("adaw", [128, NCH, 512], BF16, 3)
            kb.dma("pool", wa[:], wv[:, :, blk * 512:(blk + 1) * 512])
            ps = kb.bank()
            for c in range(NCH):
                kb.mm(ps[0:2, :], svb[:, c, :], wa[:, c, :], start=(c == 0), stop=(c == NCH - 1))
            o0 = l * 12288 + blk * 512
            kb.tt("dve", rows[:, o0:o0 + 512], ps[0:2, :], rows[:, o0:o0 + 512], ALU.add)
    for half in range(2):
        tps = kb.bank()
        for gch in range(96):
            gg = half * 96 + gch
            kb.mm(tps[:, gch * 2:gch * 2 + 2], rows[:, gg * 128:(gg + 1) * 128], i2[:])
        kb.copy("dve", modG[:, half * 96:(half + 1) * 96, :].rearrange("p a b -> p (a b)"), tps[:, 0:192])
    kb.close_scope()

    kb.open_outer()
    sh = phase_a_shared(nc, kb, io, ng=4)
    ymix_all = nc.dram_tensor("ymix_all", [4, 4, 128, T], BF16).ap()

    def modsrc(A1v, B1v):
        kb.open_scope()
        nmix = kb.sb("nmix", [128, NCH])
        kb.dma("sp", nmix[:], io["nmix"])
        kb.copy("dve", B1v[:], modG[:, 0:16, :])
        kb.ts("dve", A1v[:], modG[:, 16:32, :], 1.0, ALU.add)
        kb.tt("dve", A1v[:], A1v[:], nmix[:].unsqueeze(2).to_broadcast([128, NCH, 2]), ALU.mult)
        kb.close_scope()

    for g in range(4):
        phase_a(nc, kb, io, sh=sh, g=g, modsrc=(modsrc if g == 0 else "done"), ymix_dst=ymix_all[g], do_scan=False)
    for pair in ((0, 1), (2, 3)):
        phase_a_scan(nc, kb, io, sh, [(g, g) for g in pair], [ymix_all[g] for g in pair])

    kb.close_outer()

    def fill(bufA, modB):
        jsel = kb.sb("jsel", [128, 4])
        kb.dma("sp", jsel[:], io["jsel"])
        kb.open_scope()
        for j in range(4):
            cand = kb.tmp("cand", [128, NCH, NT], BF16, 1)
            lo = j * 1024 - 1
            hi = j * 1024 + 1025
            slo, shi = max(lo, 0), min(hi, T)
            d0 = slo - lo
            if d0 > 0:
                kb.memset("pool", cand[:, :, 0:d0], 0.0)
            if shi < hi:
                kb.memset("pool", cand[:, :, NT - (hi - shi):NT], 0.0)
            for ch in range(NCH):
                h = ch % 8
                gsrc, idx = h // 2, (h % 2) + (0 if ch < 8 else 2)
                kb.dma("sp", cand[:, ch, d0:d0 + (shi - slo)], ymix_all[gsrc, idx][:, slo:shi])
            if j == 0:
                kb.ts("dve", bufA[:], cand[:], jsel[:, 0:1], ALU.mult)
            else:
                kb.stt("dve", bufA[:], cand[:], jsel[:, j:j + 1], bufA[:], ALU.mult, ALU.add)
        kb.close_scope()
        for i, blk in enumerate((2, 3, 4, 5, 6, 7, 8, 9, 10, 11)):
            off = blk * 16 if i < 4 else 96 + (blk - 6) * 16
            kb.copy("dve", modB[:, i, :], modG[:, off:off + 16, 0])

    phase_b(nc, kb, io, fill=fill)


T = 4096
L = 256
EVEN_OFF = dict(ka=0, va=1024, kb=2048, vb=3072, a=4096, b=4112, qa=4128, qb=5152, za=6176)


def pc_layout(v):
    return np.ascontiguousarray(v.reshape(16, 128).T)


def make_consts():
    ident = np.eye(128, dtype=np.float32)
    t = np.arange(128)
    ch = t // 64
    same = ch[:, None] == ch[None, :]
    m = np.zeros((13, 128, 128), np.float32)
    m[0] = same & (t[:, None] <= t[None, :])
    m[1] = same & (t[:, None] >= t[None, :])
    m[2] = (t[:, None] == (64 * ch[None, :] + 63))
    m[3] = (t[:, None] == (64 * ch[None, :]))
    for k, r in enumerate((63, 127, 0, 64)):
        m[4 + k][r, :] = 1.0
    NEG = -1.0e4
    j = t[:, None]
    i = t[None, :]
    m[8] = np.where(same & (i >= j), 0.0, NEG)
    m[9] = np.where(same & (i > j), 0.0, NEG)
    m[10] = np.where(same & (i <= j), 0.0, NEG)
    m[11] = np.where(same & (i < j), 0.0, NEG)
    R = np.zeros((128, 128), np.float32)
    for dp in range(64):
        R[dp + 64, dp] = -1.0
    for dp in range(64, 128):
        R[dp - 64, dp] = 1.0
    m[12] = R
    cmask = np.ascontiguousarray(m.transpose(1, 0, 2))
    tt = np.arange(T)
    row = (tt // 64).astype(np.float32)
    col = (tt % 64).astype(np.float32)
    inv_freq = (np.float32(10000.0) ** (-np.arange(32, dtype=np.float32) / np.float32(32))).astype(np.float32)
    ang = np.concatenate([row[:, None] * inv_freq, col[:, None] * inv_freq], axis=-1).astype(np.float32)
    cos = np.cos(ang).astype(np.float32).T
    sin = np.sin(ang).astype(np.float32).T
    cos = np.ascontiguousarray(np.concatenate([cos, cos], 0))
    sin = np.ascontiguousarray(np.concatenate([sin, sin], 0))
    return dict(ident=ident, cmask=cmask, cos=cos, sin=sin)


def make_nabias(rpb_h):
    out = np.full((5, 2, 128, 640), -30000.0, np.float32)
    cq = np.arange(64)
    kc = np.arange(64)
    cs = np.clip(cq - 8, 0, 48)
    valid_col = (kc[None, :] >= cs[:, None]) & (kc[None, :] < cs[:, None] + 16)
    dc = np.clip(kc[None, :] - cq[:, None], -15, 15) + 15
    for ty, m in enumerate((0, 1, 2, 30, 31)):
        base_row = 2 * min(max(m - 2, 0), 27)
        for rr in range(2):
            r = 2 * m + rr
            rs = min(max(r - 4, 0), 56)
            for jrow in range(10):
                krow = base_row + jrow
                if not (rs <= krow < rs + 8):
                    continue
                dr = krow - r + 7
                for hh in range(2):
                    blk = np.where(valid_col, rpb_h[hh][dr][dc], np.float32(-30000.0))
                    out[ty, hh, rr * 64:(rr + 1) * 64, jrow * 64:(jrow + 1) * 64] = blk
    return out


def group_weights(inp, g):
    w_in = inp["ev_w_in"][0]
    heads = [2 * g, 2 * g + 1]

    def cols(seg, h):
        o = EVEN_OFF[seg] + h * 128
        return w_in[:, o:o + 128]

    wf = np.concatenate([cols(s, h) for s in ("ka", "va", "qa", "za", "kb", "qb") for h in heads], axis=1)
    gate_cols = []
    for seg in ("a", "b"):
        for d in range(2):
            for h in heads:
                o = EVEN_OFF[seg] + d * 8 + h
                gate_cols.append(w_in[:, o:o + 1])
    wt = np.concatenate([cols("vb", h) for h in heads] + gate_cols, axis=1)
    conv = inp["ev_conv"][0]
    convw = np.zeros((128, 3, 2, 5), np.float32)
    for seg in range(3):
        for hh, h in enumerate(heads):
            convw[:, seg, hh, :] = conv[:, seg * 1024 + h * 128: seg * 1024 + (h + 1) * 128].T
    a_log = inp["ev_a_log"][0]
    dtb = inp["ev_dt_bias"][0]
    gp = np.zeros((8,), np.float32)
    for d in range(2):
        for hh, h in enumerate(heads):
            gp[d * 2 + hh] = a_log[d, h]
            gp[4 + d * 2 + hh] = dtb[d, h]
    gpar = np.ascontiguousarray(np.broadcast_to(gp[None, :], (128, 8))).astype(np.float32)
    return dict(wf=np.ascontiguousarray(wf, dtype=np.float32), wt=np.ascontiguousarray(wt, dtype=np.float32),
                convw=convw, gpar=gpar, nabias=make_nabias(inp["ev_rpb"][0][heads]))


def host_a(inp, core, consts, mod):
    b, g = core // 4, core % 4
    modA = np.zeros((128, 2, 16, 2), np.float32)
    for blk in range(2):
        modA[:, blk, :, 0] = pc_layout(mod[b, 0, blk * 2048:(blk + 1) * 2048])
        modA[:, blk, :, 1] = pc_layout(mod[2, 0, blk * 2048:(blk + 1) * 2048])
    m = dict(
        xT=np.ascontiguousarray(inp["x"][b].T),
        ctxT=np.ascontiguousarray(inp["ctx"][b].T),
        modA=modA,
        nmix=pc_layout(inp["norm_mix"][0]),
        gnorm=np.ascontiguousarray(inp["ev_gdn_norm"][0].reshape(128, 1)),
    )
    m.update(group_weights(inp, g))
    m.update(consts)
    return {k: np.ascontiguousarray(v, dtype=np.float32) for k, v in m.items()}


def host_f(inp, core, consts, gw_all, xT_b, ctxT_b):
    b, j = core // 4, core % 4
    t0 = j * 1024
    x = inp["x"][b]
    xTo = np.zeros((2048, 1026), np.float32)
    xTo[:, 1:1025] = x[t0:t0 + 1024].T
    hmask = np.zeros((128, 2), np.float32)
    if t0 > 0:
        xTo[:, 0] = x[t0 - 1]
        hmask[:, 0] = 1.0
    if t0 + 1024 < 4096:
        xTo[:, 1025] = x[t0 + 1024]
        hmask[:, 1] = 1.0
    jsel = np.zeros((128, 4), np.float32)
    jsel[:, j] = 1.0
    norms = np.stack([pc_layout(inp["norm_ffn"][0]), pc_layout(inp["norm_mix"][1]),
                      pc_layout(inp["norm_ffn"][1]), pc_layout(inp["final_norm"])], axis=1)
    conv1 = np.stack([pc_layout(inp["od_conv"][0][k]) for k in range(3)], axis=-1)
    svec = np.stack([pc_layout(inp["c"][b]), pc_layout(inp["c_ctx"])], axis=-1)
    f = lambda a: np.ascontiguousarray(a, dtype=np.float32)
    m = dict(
        xT=xT_b[b], ctxT=ctxT_b[b], nmix=pc_layout(inp["norm_mix"][0]),
        gnorm=f(inp["ev_gdn_norm"][0].reshape(128, 1)),
        svec=f(svec), adaw=f(inp["ada_w"]), adab=f(inp["ada_b"]), i2=np.eye(2, dtype=np.float32),
        xTo=f(xTo), hmask=f(hmask), jsel=f(jsel), norms=f(norms),
        w_out0=f(inp["ev_w_out"][0]), wg0=f(inp["ffn_w_gate"][0]), wu0=f(inp["ffn_w_up"][0]), wd0=f(inp["ffn_w_down"][0]),
        w_in1=f(inp["od_w_in"][0]), conv1=f(conv1), w_out1=f(inp["od_w_out"][0]),
        wg1=f(inp["ffn_w_gate"][1]), wu1=f(inp["ffn_w_up"][1]), wd1=f(inp["ffn_w_down"][1]),
    )
    m.update(gw_all)
    m.update(consts)
    return {k: np.ascontiguousarray(v, dtype=np.float32) for k, v in m.items()}


D = 2048


def host_ada(inp, core):
    l, off = divmod(core * 3072, 12288)
    svec = np.stack([pc_layout(inp["c"][0]), pc_layout(inp["c"][1]), pc_layout(inp["c_ctx"])], axis=-1)
    return dict(
        svec=np.ascontiguousarray(svec, dtype=np.float32),
        adaw=np.ascontiguousarray(inp["ada_w"][l][:, off:off + 3072]),
        adab=np.ascontiguousarray(inp["ada_b"][l][off:off + 3072].reshape(1, 3072)),
    )


def assemble_mod(res_list):
    full = np.concatenate([np.asarray(r["mod"], dtype=np.float32) for r in res_list], axis=1)
    return full.reshape(3, 2, 12288)


def mod_pc(v):
    return pc_layout(v)


def host_b(inp, core, mod, ymix_all):
    b, j = core // 4, core % 4
    t0 = j * 1024
    x = inp["x"][b]
    xT = np.zeros((D, 1026), np.float32)
    xT[:, 1:1025] = x[t0:t0 + 1024].T
    hmask = np.zeros((128, 2), np.float32)
    ym = np.zeros((D, 1026), ml_dtypes.bfloat16)

    def ycols(lo, hi):
        out = np.zeros((D, hi - lo), ml_dtypes.bfloat16)
        for g in range(4):
            r = ymix_all[b * 4 + g]
            for hh in range(2):
                h = 2 * g + hh
                out[h * 128:(h + 1) * 128] = r[hh][:, lo:hi]
                out[1024 + h * 128:1024 + (h + 1) * 128] = r[2 + hh][:, lo:hi]
        return out

    ym[:, 1:1025] = ycols(t0, t0 + 1024)
    if t0 > 0:
        xT[:, 0] = x[t0 - 1]
        ym[:, 0:1] = ycols(t0 - 1, t0)
        hmask[:, 0] = 1.0
    if t0 + 1024 < 4096:
        xT[:, 1025] = x[t0 + 1024]
        ym[:, 1025:1026] = ycols(t0 + 1024, t0 + 1025)
        hmask[:, 1] = 1.0
    m0 = mod[b, 0].reshape(6, D)
    m1 = mod[b, 1].reshape(6, D)
    sel = [m0[2], m0[3], m0[4], m0[5], m1[0], m1[1], m1[2], m1[3], m1[4], m1[5]]
    modB = np.stack([pc_layout(v) for v in sel], axis=1)
    norms = np.stack([pc_layout(inp["norm_ffn"][0]), pc_layout(inp["norm_mix"][1]),
                      pc_layout(inp["norm_ffn"][1]), pc_layout(inp["final_norm"])], axis=1)
    conv1 = np.stack([pc_layout(inp["od_conv"][0][k]) for k in range(3)], axis=-1)
    f = lambda a: np.ascontiguousarray(a, dtype=np.float32)
    return dict(
        xT=f(xT), ymixT=np.ascontiguousarray(ym), hmask=f(hmask), modB=f(modB), norms=f(norms),
        w_out0=f(inp["ev_w_out"][0]), wg0=f(inp["ffn_w_gate"][0]), wu0=f(inp["ffn_w_up"][0]), wd0=f(inp["ffn_w_down"][0]),
        w_in1=f(inp["od_w_in"][0]), conv1=f(conv1), w_out1=f(inp["od_w_out"][0]),
        wg1=f(inp["ffn_w_gate"][1]), wu1=f(inp["ffn_w_up"][1]), wd1=f(inp["ffn_w_down"][1]),
    )

def _build(io_fn, prog_fn):
    nc = bass.Bass("TRN2", target_bir_lowering=False)
    io = io_fn(nc)
    with ExitStack() as es:
        kb = KB(nc, es)
        prog_fn(nc, kb, io)
        kb.p.emit()
    return nc


def kernel(**inputs):
    inp = {k: np.asarray(v) for k, v in inputs.items()}
    cores = list(range(8))
    consts = make_consts()
    gws = [group_weights(inp, g) for g in range(4)]
    gw_all = {k: np.stack([gw[k] for gw in gws]) for k in gws[0]}
    xT_b = [np.ascontiguousarray(inp["x"][b].T) for b in range(2)]
    ctxT_b = [np.ascontiguousarray(inp["ctx"][b].T) for b in range(2)]
    nc = _build(fused_io, fused_prog)
    in_maps = [host_f(inp, k, consts, gw_all, xT_b, ctxT_b) for k in cores]
    res = run_bass_kernel_spmd(nc, in_maps, core_ids=cores)
    out = np.zeros((2, 4096, 2048), np.float32)
    for k in cores:
        b, j = k // 4, k % 4
        out[b, j * 1024:(j + 1) * 1024, :] = np.asarray(res.results[k]["outT"]).T
    return out
```

```python
import os
import numpy as np
import ml_dtypes
from contextlib import ExitStack
import concourse.bass as bass
import concourse.mybir as mybir
from concourse.bass_utils import run_bass_kernel_spmd


F32 = mybir.dt.float32
BF16 = mybir.dt.bfloat16
AF = mybir.ActivationFunctionType
ALU = mybir.AluOpType
AX = mybir.AxisListType
R32 = mybir.dt.float32r
DTSIZE = {F32: 4, BF16: 2, R32: 4}

ENGS = ("pe", "act", "dve", "pool", "sp")
EPOCH = 12000
NDSEM = {"sp": 16, "pool": 8, "act": 8}
DQ = ("sp", "pool", "act")


def ap_box(ap):
    es = DTSIZE[ap.dtype]
    pat = ap.ap
    off = ap.offset
    name = ap.tensor.name
    if str(ap.space) == "DRAM":
        hi = off
        for (st, cnt) in pat:
            hi += st * (cnt - 1)
        return (name, 0, 1, off * es, (hi + 1) * es)
    row = pat[0][0]
    if row == 0:
        row = 1 << 40
    p0 = off // row
    f0 = off % row
    p1 = p0 + pat[0][1]
    hi = f0
    for (st, cnt) in pat[1:]:
        hi += st * (cnt - 1)
    if str(ap.space) == "PSUM":
        return (name, (p0 // 32) * 32, ((p1 + 31) // 32) * 32, 0, 1 << 20)
    return (name, p0, p1, f0 * es, (hi + 1) * es)


class Prog:
    def __init__(self, nc):
        self.nc = nc
        self.ops = {e: [] for e in ENGS}
        self.cnt = {e: 0 for e in ENGS if e != "sp"}
        self.seen = {e: {} for e in ENGS}
        self.recs = {}
        self.ndma = {q: 0 for q in DQ}
        self.dma_tok = {q: [None] * NDSEM[q] for q in DQ}

    @staticmethod
    def _ov(a, b):
        return a[1] < b[2] and b[1] < a[2] and a[3] < b[4] and b[3] < a[4]

    @staticmethod
    def _covers(a, b):
        return a[1] <= b[1] and a[2] >= b[2] and a[3] <= b[3] and a[4] >= b[4]

    def _deps(self, reads, writes):
        deps = []
        for ap in reads:
            b = ap_box(ap)
            psum = str(ap.space) == "PSUM"
            for (ob, tok, isw) in self.recs.get(b[0], ()):
                if (isw or psum) and self._ov(b, ob):
                    deps.append(tok)
        for ap in writes:
            b = ap_box(ap)
            for (ob, tok, isw) in self.recs.get(b[0], ()):
                if self._ov(b, ob):
                    deps.append(tok)
        return deps

    def _record(self, reads, writes, tok):
        for ap in writes:
            b = ap_box(ap)
            lst = self.recs.setdefault(b[0], [])
            lst[:] = [r for r in lst if not self._covers(b, r[0])]
            lst.append([b, tok, True])
        for ap in reads:
            b = ap_box(ap)
            lst = self.recs.setdefault(b[0], [])
            if tok[0] == "e":
                lst[:] = [r for r in lst if not ((not r[2]) and r[1][0] == "e" and r[1][1] == tok[1]
                                                 and self._covers(b, r[0]))]
            lst.append([b, tok, False])

    def _waits(self, eng, deps, skip_same=False):
        need = {}
        for tok in deps:
            if tok[0] == "e":
                if tok[1] == eng and skip_same:
                    continue
                key = ("e", tok[1])
            else:
                key = ("d", tok[1])
            v = tok[2]
            if self.seen[eng].get(key, 0) >= v:
                continue
            if need.get(key, 0) < v:
                need[key] = v
        for k, v in need.items():
            self.seen[eng][k] = v
        return list(need.items())

    def op(self, eng, fn, reads=(), writes=(), skip_same=False):
        deps = self._deps(reads, writes)
        waits = self._waits(eng, deps, skip_same)
        self.cnt[eng] += 1
        idx = self.cnt[eng]
        tok = ("e", eng, idx)
        self._record(reads, writes, tok)
        self.ops[eng].append(("op", fn, waits, idx))
        return tok

    def dma(self, queue, out, in_):
        deps = self._deps([in_], [out])
        ns = NDSEM[queue]
        slot = self.ndma[queue] % ns
        val = 16 * (self.ndma[queue] // ns + 1)
        self.ndma[queue] += 1
        prev = self.dma_tok[queue][slot]
        if prev is not None:
            deps.append(prev)
        waits = self._waits(queue, deps)
        tok = ("d", (queue, slot), val)
        self.dma_tok[queue][slot] = tok
        self._record([in_], [out], tok)
        self.ops[queue].append(("dma", (out, in_), waits, ((queue, slot), val)))
        return tok

    def barrier(self):
        deps = [("e", e, self.cnt[e]) for e in self.cnt if self.cnt[e] > 0]
        deps += [t for q in DQ for t in self.dma_tok[q] if t is not None]
        for eng in ENGS:
            waits = self._waits(eng, deps, skip_same=True)
            if waits:
                self.ops[eng].append(("wait", None, waits, None))
        self.recs = {}

    def emit(self):
        nc = self.nc
        with ExitStack() as es:
            esem = {}
            for e in self.cnt:
                n_ep = self.cnt[e] // EPOCH + 1
                esem[e] = [es.enter_context(nc.semaphore(f"s_{e}_{k}")) for k in range(n_ep)]
            dsem = {(q, k): es.enter_context(nc.semaphore(f"s_dma_{q}_{k}")) for q in DQ for k in range(NDSEM[q])
                    if self.ndma[q] > k}
            block = es.enter_context(nc.Block())

            def semval(key, v):
                if key[0] == "e":
                    ep = (v - 1) // EPOCH
                    return esem[key[1]][ep], v - ep * EPOCH
                return dsem[key[1]], v

            def run(engname, engobj):
                for (kind, payload, waits, info) in self.ops[engname]:
                    for (key, v) in waits:
                        s, sv = semval(key, v)
                        engobj.wait_ge(s, sv)
                    if kind == "op":
                        ins = payload(engobj)
                        s, sv = semval(("e", engname), info)
                        ins.then_inc(s, 1)
                    elif kind == "dma":
                        out, in_ = payload
                        ins = engobj.dma_start(out=out, in_=in_)
                        ins.then_inc(dsem[info[0]], 16)
                if engname == "sp":
                    for q in DQ:
                        for tok in self.dma_tok[q]:
                            if tok is not None:
                                engobj.wait_ge(dsem[tok[1]], tok[2])
                    for e in self.cnt:
                        if self.cnt[e] > 0:
                            s, sv = semval(("e", e), self.cnt[e])
                            engobj.wait_ge(s, sv)

            @block.tensor
            def _(eng):
                run("pe", eng)

            @block.scalar
            def _(eng):
                run("act", eng)

            @block.vector
            def _(eng):
                run("dve", eng)

            @block.gpsimd
            def _(eng):
                run("pool", eng)

            @block.sync
            def _(eng):
                run("sp", eng)


def _isap(x):
    return not isinstance(x, (int, float)) and x is not None


class KB:
    def __init__(self, nc, es):
        self.nc = nc
        self.es = es
        self.p = Prog(nc)
        self.banks = [es.enter_context(nc.psum_tensor(f"psb{i}", [128, 512], F32)) for i in range(6)]
        self.bbanks = [es.enter_context(nc.psum_tensor(f"psh{i}", [128, 1024], BF16)) for i in range(2)]
        self.bi = 0
        self.bbi = 0
        self.pools = {}
        self.scope = None
        self.outer = None
        self.nname = 0

    def bank(self):
        b = self.banks[self.bi % len(self.banks)]
        self.bi += 1
        return b

    def bbank(self):
        b = self.bbanks[self.bbi % len(self.bbanks)]
        self.bbi += 1
        return b

    def sb(self, name, shape, dt=F32):
        es = self.scope if self.scope is not None else (self.outer if self.outer is not None else self.es)
        self.nname += 1
        return es.enter_context(self.nc.sbuf_tensor(f"{name}_{self.nname}", list(shape), dt))

    def tmp(self, pool, shape, dt=F32, nbuf=2):
        if pool not in self.pools:
            self.pools[pool] = ([self.sb(f"{pool}{i}", shape, dt) for i in range(nbuf)], [0])
        bufs, ctr = self.pools[pool]
        t = bufs[ctr[0] % len(bufs)]
        ctr[0] += 1
        return t

    def open_outer(self):
        assert self.outer is None and self.scope is None
        self.outer = ExitStack()
        self.outer.__enter__()

    def close_outer(self):
        assert self.scope is None
        self.p.barrier()
        self.outer.__exit__(None, None, None)
        self.outer = None

    def open_scope(self):
        assert self.scope is None
        self.scope = ExitStack()
        self.scope.__enter__()
        self.pools = {}

    def close_scope(self):
        self.p.barrier()
        self.scope.__exit__(None, None, None)
        self.scope = None
        self.pools = {}

    def mm(self, out, lhsT, rhs, start=True, stop=True):
        self.p.op("pe", lambda e: e.matmul(out, lhsT=lhsT, rhs=rhs, start=start, stop=stop),
                  reads=[lhsT, rhs], writes=[out], skip_same=True)

    def tr(self, out, in_, ident):
        self.p.op("pe", lambda e: e.transpose(out, in_, ident), reads=[in_, ident], writes=[out], skip_same=True)

    def act(self, out, in_, func, bias=None, scale=None, accum_out=None):
        kw = {}
        reads = [in_]
        writes = [out]
        if bias is not None:
            kw["bias"] = bias
            if _isap(bias):
                reads.append(bias)
        if scale is not None:
            kw["scale"] = scale
            if _isap(scale):
                reads.append(scale)
        if accum_out is not None:
            kw["accum_out"] = accum_out
            writes.append(accum_out)
        self.p.op("act", lambda e: e.activation(out=out, in_=in_, func=func, **kw), reads=reads, writes=writes)

    def tt(self, eng, out, in0, in1, op):
        self.p.op(eng, lambda e: e.tensor_tensor(out=out, in0=in0, in1=in1, op=op), reads=[in0, in1], writes=[out])

    def ts(self, eng, out, in0, s1, op0, s2=None, op1=None):
        reads = [in0] + [s for s in (s1, s2) if _isap(s)]
        if op1 is None:
            self.p.op(eng, lambda e: e.tensor_scalar(out=out, in0=in0, scalar1=s1, scalar2=None, op0=op0),
                      reads=reads, writes=[out])
        else:
            self.p.op(eng, lambda e: e.tensor_scalar(out=out, in0=in0, scalar1=s1, scalar2=s2, op0=op0, op1=op1),
                      reads=reads, writes=[out])

    def stt(self, eng, out, in0, scalar, in1, op0, op1):
        reads = [in0, in1] + ([scalar] if _isap(scalar) else [])
        self.p.op(eng, lambda e: e.scalar_tensor_tensor(out=out, in0=in0, scalar=scalar, in1=in1, op0=op0, op1=op1),
                  reads=reads, writes=[out])

    def copy(self, eng, out, in_):
        if eng == "act":
            self.p.op("act", lambda e: e.copy(out=out, in_=in_), reads=[in_], writes=[out])
        else:
            self.p.op(eng, lambda e: e.tensor_copy(out=out, in_=in_), reads=[in_], writes=[out])

    def memset(self, eng, ap, val):
        self.p.op(eng, lambda e: e.memset(ap, val), reads=[], writes=[ap])

    def recip(self, out, in_):
        self.p.op("dve", lambda e: e.reciprocal(out=out, in_=in_), reads=[in_], writes=[out])

    def rmax(self, out, in_):
        self.p.op("dve", lambda e: e.reduce_max(out=out, in_=in_, axis=AX.X), reads=[in_], writes=[out])

    def dma(self, q, out, in_):
        self.p.dma(q, out, in_)

STOP = os.environ.get('STOP_AFTER', '')
TMN = int(os.environ.get('TMN', '264'))

T = 4096
L = 256
D = 2048
NCH = 16
EPS = 1e-6
SCALE = 128 ** -0.5

M_UF, M_UB, M_SELF, M_SELB, M_R63, M_R127, M_R0, M_R64, M_AIF, M_ASF, M_AIB, M_ASB, M_ROT = range(13)
NMASK = 13


def phase_a_io(nc):
    io = {}

    def I(name, shape, dt=F32):
        io[name] = nc.dram_tensor(name, list(shape), dt, kind="ExternalInput").ap()

    I("xT", [D, T]); I("ctxT", [D, L]); I("modA", [128, 2, NCH, 2])
    I("nmix", [128, NCH]); I("wf", [D, 1536]); I("wt", [D, 264]); I("convw", [128, 3, 2, 5]); I("gpar", [128, 8])
    I("gnorm", [128, 1]); I("nabias", [5, 2, 128, 640]); I("ident", [128, 128]); I("cmask", [128, NMASK, 128])
    I("cos", [128, T]); I("sin", [128, T])
    io["ymix"] = nc.dram_tensor("ymix", [4, 128, T], BF16, kind="ExternalOutput").ap()
    return io


def gate_core(kb, masks, gsrc, NT, tag, gp_, nega_):
    G = kb.sb("G" + tag, [128, NT, 8])
    kb.dma("sp", G[:], gsrc)
    t1 = kb.sb("t1" + tag, [128, 4, NT])
    kb.tt("dve", t1[:], G[:, :, 0:4].rearrange("p n c -> p c n"),
          gp_[:, 4:8].unsqueeze(2).to_broadcast([128, 4, NT]), ALU.add)
    kb.act(t1[:], t1[:], AF.Exp)
    kb.act(t1[:], t1[:], AF.Ln, bias=1.0)
    g = kb.sb("g" + tag, [128, 4, NT])
    kb.tt("dve", g[:], t1[:], nega_[:].unsqueeze(2).to_broadcast([128, 4, NT]), ALU.mult)
    gcps = kb.bank()
    kb.mm(gcps[:, 0:2 * NT], masks[:, M_UF, :], g[:, 0:2, :].rearrange("p c n -> p (c n)"))
    kb.mm(gcps[:, 2 * NT:4 * NT], masks[:, M_UB, :], g[:, 2:4, :].rearrange("p c n -> p (c n)"))
    gc = kb.sb("gc" + tag, [128, 4, NT])
    kb.copy("dve", gc[:].rearrange("p c n -> p (c n)"), gcps[:, 0:4 * NT])
    bcps = kb.bank()
    for q, (rf, rb) in enumerate(((M_R63, M_R0), (M_R127, M_R64))):
        kb.mm(bcps[:, (q * 4) * NT:(q * 4 + 2) * NT], masks[:, rf, :], gc[:, 0:2, :].rearrange("p c n -> p (c n)"))
        kb.mm(bcps[:, (q * 4 + 2) * NT:(q * 4 + 4) * NT], masks[:, rb, :], gc[:, 2:4, :].rearrange("p c n -> p (c n)"))
    egl = kb.sb("egl" + tag, [128, 2, 4, NT])
    kb.act(egl[:].rearrange("p q c n -> p (q c n)"), bcps[:, 0:8 * NT], AF.Exp)
    return G, gc, egl


def phase_a_shared(nc, kb, io, ng=1):
    sh = {}

    def scr(name, shape, dt=F32):
        return nc.dram_tensor(name, list(shape), dt).ap()

    sh["ng"] = ng
    sh["fm32"] = scr("fm32", [ng, 8, 128, T])
    sh["fm32c"] = scr("fm32c", [4, 128, L])
    sh["fmbf"] = scr("fmbf", [4, 128, T], BF16)
    sh["fmbfc"] = scr("fmbfc", [2, 128, L], BF16)
    sh["vbs"] = scr("vbs", [128, 32, 256], BF16)
    sh["vbsc"] = scr("vbsc", [128, 2, 256], BF16)
    sh["gts"] = scr("gts", [ng, 128, 32, 8])
    sh["gtsc"] = scr("gtsc", [ng, 128, 2, 8])
    sh["spill"] = scr("spill", [ng, 4, 32, 5, 128, 128])
    sh["spillc"] = scr("spillc", [ng, 4, 2, 5, 128, 128])
    ident = kb.sb("ident", [128, 128])
    kb.dma("sp", ident[:], io["ident"])
    identb = kb.sb("identb", [128, 128], BF16)
    kb.dma("pool", identb[:], io["ident"])
    ones = kb.sb("ones", [128, 128])
    kb.memset("dve", ones[:], 1.0)
    onesb = kb.sb("onesb", [128, 128], BF16)
    kb.memset("dve", onesb[:], 1.0)
    masks = kb.sb("masks", [128, NMASK, 128])
    kb.dma("sp", masks[:], io["cmask"])
    onesr = kb.sb("onesr", [128, 128], R32)
    kb.ts("dve", onesr[:], ident[:], 0.0, ALU.mult, 1.0, ALU.add)
    masksr = kb.sb("masksr", [128, NMASK, 128], R32)
    kb.copy("dve", masksr[:], masks[:])
    sh.update(ident=ident, identb=identb, ones=ones, onesb=onesb, masks=masks, onesr=onesr, masksr=masksr)
    sh["A1v"] = kb.sb("A1v", [128, NCH, 2])
    sh["B1v"] = kb.sb("B1v", [128, NCH, 2])
    return sh


def phase_a(nc, kb, io, dbg=None, sh=None, g=None, modsrc=None, ymix_dst=None, do_scan=True):
    p = kb.p
    if sh is None:
        sh = phase_a_shared(nc, kb, io)
    gidx = 0 if (g is None or sh["ng"] == 1) else g
    fm32, fm32c, fmbf, fmbfc = sh["fm32"][gidx], sh["fm32c"], sh["fmbf"], sh["fmbfc"]
    vbs, vbsc, gts, gtsc, spill, spillc = sh["vbs"], sh["vbsc"], sh["gts"][gidx], sh["gtsc"][gidx], sh["spill"][gidx], sh["spillc"][gidx]
    ident, identb, ones, onesb, masks = sh["ident"], sh["identb"], sh["ones"], sh["onesb"], sh["masks"]
    onesr, masksr = sh["onesr"], sh["masksr"]
    A1v, B1v = sh["A1v"], sh["B1v"]
    W = (lambda name: io[name]) if g is None else (lambda name: io[name][g])
    ymix = io["ymix"] if ymix_dst is None else ymix_dst

    if modsrc is None:
        kb.open_scope()
        modA = kb.sb("modA", [128, 2, NCH, 2])
        kb.dma("sp", modA[:], io["modA"])
        nmix = kb.sb("nmix", [128, NCH])
        kb.dma("sp", nmix[:], io["nmix"])
        kb.copy("dve", B1v[:], modA[:, 0, :, :])
        kb.ts("dve", A1v[:], modA[:, 1, :, :], 1.0, ALU.add)
        kb.tt("dve", A1v[:], A1v[:], nmix[:].unsqueeze(2).to_broadcast([128, NCH, 2]), ALU.mult)
        kb.close_scope()
    elif modsrc != "done":
        modsrc(A1v, B1v)
    if STOP == "A0":
        return
    kb.open_scope()
    wfb = kb.sb("wfb", [128, NCH, 1536], BF16)
    wf_v = W("wf").rearrange("(c p) n -> p c n", p=128)
    for i in range(3):
        kb.dma("pool", wfb[:, :, i * 512:(i + 1) * 512], wf_v[:, :, i * 512:(i + 1) * 512])
    wtb = kb.sb("wtb", [128, NCH, 264], BF16)
    kb.dma("pool", wtb[:], W("wt").rearrange("(c p) n -> p c n", p=128))
    xT_v = io["xT"].rearrange("(c p) t -> p c t", p=128)
    cT_v = io["ctxT"].rearrange("(c p) t -> p c t", p=128)
    groups = [("c", 0, L)] + [("l", g * 512, 512) for g in range(8)]
    evac_i = 0
    def a1_prep(kind, t0, n):
        w = 1 if kind == "c" else 0
        xt = kb.tmp("xt", [128, NCH, 512], F32, 2)
        src = cT_v if kind == "c" else xT_v[:, :, t0:t0 + n]
        kb.dma("sp", xt[:, :, 0:n], src)
        ssq = kb.bank()
        for c in range(NCH):
            sq = kb.tmp("sq", [128, 512], BF16, 3)
            kb.act(sq[:, 0:n], xt[:, c, 0:n], AF.Square)
            kb.mm(ssq[:, 0:n], onesb[:], sq[:, 0:n], start=(c == 0), stop=(c == NCH - 1))
        rn = kb.tmp("rn", [128, 512], F32, 2)
        kb.act(rn[:, 0:n], ssq[:, 0:n], AF.Ln, bias=EPS, scale=1.0 / D)
        kb.act(rn[:, 0:n], rn[:, 0:n], AF.Exp, scale=-0.5)
        hT = kb.tmp("hT", [128, NCH, 512], BF16, 2)
        for c in range(NCH):
            tf = kb.tmp("tf", [128, 512], F32, 3)
            kb.tt("dve", tf[:, 0:n], xt[:, c, 0:n], rn[:, 0:n], ALU.mult)
            kb.act(hT[:, c, 0:n], tf[:, 0:n], AF.Identity, bias=B1v[:, c, w:w + 1], scale=A1v[:, c, w:w + 1])
        return hT

    hT_next = a1_prep(*groups[0])
    for gi, (kind, t0, n) in enumerate(groups):
        hT = hT_next
        if gi + 1 < len(groups):
            hT_next = a1_prep(*groups[gi + 1])
        cbs = [0, 1, 2, 3, 8, 9] if kind == "c" else list(range(12))
        for cb in cbs:
            ps = kb.bank()
            for c in range(NCH):
                kb.mm(ps[:, 0:n], wfb[:, c, cb * 128:(cb + 1) * 128], hT[:, c, 0:n], start=(c == 0), stop=(c == NCH - 1))
            eng = "act" if evac_i % 2 == 0 else "dve"
            evac_i += 1
            if cb < 8:
                so = kb.tmp("so32", [128, 512], F32, 3)
                kb.copy(eng, so[:, 0:n], ps[:, 0:n])
                dst = fm32c[cb][:, :] if kind == "c" else fm32[cb][:, t0:t0 + n]
                kb.dma("pool", dst, so[:, 0:n])
            else:
                so = kb.tmp("sobf", [128, 512], BF16, 3)
                kb.copy(eng, so[:, 0:n], ps[:, 0:n])
                dst = fmbfc[cb - 8][:, :] if kind == "c" else fmbf[cb - 8][:, t0:t0 + n]
                kb.dma("pool", dst, so[:, 0:n])
        nsub = n // 128
        sg_ = kb.tmp("sogt", [128, 4, 8], F32, 2)
        for s in range(nsub):
            ps = kb.bank()
            for c in range(NCH):
                kb.mm(ps[:, 0:TMN], hT[:, c, s * 128:(s + 1) * 128], wtb[:, c, 0:TMN], start=(c == 0), stop=(c == NCH - 1))
            sv_ = kb.tmp("sovb", [128, 256], BF16, 3)
            kb.copy("act", sv_[:], ps[:, 0:256])
            kb.copy("act", sg_[:, s, :], ps[:, 256:264])
            ni = (t0 + s * 128) // 128
            if os.environ.get("SKIP_VBDMA"):
                pass
            elif kind == "c":
                kb.dma("pool", vbsc[:, ni, :], sv_[:])
            else:
                kb.dma("pool", vbs[:, ni, :], sv_[:])
        ni0 = t0 // 128
        if os.environ.get("SKIP_GTDMA"):
            pass
        elif kind == "c":
            kb.dma("pool", gtsc[:, 0:nsub, :], sg_[:, 0:nsub, :])
        else:
            kb.dma("pool", gts[:, ni0:ni0 + nsub, :], sg_[:, 0:nsub, :])
    kb.close_scope()

    if dbg is not None and "fm32" in dbg:
        kb.open_scope()
        for i in range(8):
            t_ = kb.tmp("dbgt", [128, T], F32, 1)
            kb.dma("sp", t_[:], fm32[i])
            kb.dma("sp", dbg["fm32"][i], t_[:])
        kb.close_scope()

    if STOP == "A1":
        return
    kb.open_scope()
    kbT = kb.sb("kbT", [128, 2, T], BF16)
    qbT = kb.sb("qbT", [128, 2, T], BF16)
    for hh in range(2):
        kb.dma("sp", kbT[:, hh, :], fmbf[hh])
        kb.dma("sp", qbT[:, hh, :], fmbf[2 + hh])
    vbt = kb.sb("vbt", [128, 32, 256], BF16)
    kb.dma("sp", vbt[:], vbs)
    kcx = kb.sb("kcx", [128, 2, L], BF16)
    for hh in range(2):
        kb.dma("sp", kcx[:, hh, :], fmbfc[hh])
    vcx = kb.sb("vcx", [128, 2, 256], BF16)
    kb.dma("sp", vcx[:], vbsc)
    biasT = kb.sb("biasT", [128, 5, 2, 640])
    for y in range(5):
        kb.dma("sp", biasT[:, y, :, :], W("nabias")[y].rearrange("h q k -> q h k"))
    ynaT = kb.sb("ynaT", [128, 2, T], BF16)
    for m in range(32):
        base = 128 * min(max(m - 2, 0), 27)
        typ = 0 if m == 0 else (1 if m == 1 else (2 if m <= 29 else (3 if m == 30 else 4)))
        for hh in range(2):
            q_ap = qbT[:, hh, 128 * m:128 * m + 128]
            psC = kb.bank()
            psA = kb.bank()
            psB = kb.bank()
            kb.mm(psC[:, 0:256], q_ap, kcx[:, hh, :])
            kb.mm(psA[:, 0:512], q_ap, kbT[:, hh, base:base + 512])
            kb.mm(psB[:, 0:128], q_ap, kbT[:, hh, base + 512:base + 640])
            Sb = kb.tmp("nasb", [128, 896], F32, 2)
            kb.act(Sb[:, 0:256], psC[:, 0:256], AF.Identity, scale=SCALE)
            kb.stt("dve", Sb[:, 256:768], psA[:, 0:512], SCALE, biasT[:, typ, hh, 0:512], ALU.mult, ALU.add)
            kb.stt("dve", Sb[:, 768:896], psB[:, 0:128], SCALE, biasT[:, typ, hh, 512:640], ALU.mult, ALU.add)
            mx = kb.tmp("namx", [128, 1], F32, 4)
            kb.rmax(mx[:], Sb[:])
            kb.ts("dve", mx[:], mx[:], -1.0, ALU.mult)
            P = kb.tmp("nap", [128, 896], BF16, 2)
            rs = kb.tmp("nars", [128, 1], F32, 4)
            kb.memset("dve", rs[:], 0.0)
            kb.act(P[:], Sb[:], AF.Exp, bias=mx[:], accum_out=rs[:])
            kb.recip(rs[:], rs[:])
            kb.ts("dve", P[:], P[:], rs[:], ALU.mult)
            pT = kb.bbank()
            pT2 = kb.bbank()
            for kc in range(4):
                kb.tr(pT[:, kc * 128:(kc + 1) * 128], P[:, kc * 128:(kc + 1) * 128], identb[:])
            for kc in range(4, 7):
                kb.tr(pT2[:, (kc - 4) * 128:(kc - 3) * 128], P[:, kc * 128:(kc + 1) * 128], identb[:])
            PT = kb.tmp("napt", [128, 896], BF16, 2)
            kb.copy("act", PT[:, 0:512], pT[:, 0:512])
            kb.copy("dve", PT[:, 512:896], pT2[:, 0:384])
            ops = kb.bank()
            for kc in range(7):
                if kc < 2:
                    v_ap = vcx[:, kc, hh * 128:(hh + 1) * 128]
                else:
                    v_ap = vbt[:, base // 128 + kc - 2, hh * 128:(hh + 1) * 128]
                kb.mm(ops[:, 0:128], v_ap, PT[:, kc * 128:(kc + 1) * 128], start=(kc == 0), stop=(kc == 6))
            kb.copy("act", ynaT[:, hh, 128 * m:128 * m + 128], ops[:, 0:128])
    for hh in range(2):
        kb.dma("sp", ymix[2 + hh], ynaT[:, hh, :])
    kb.close_scope()

    if STOP == "A2":
        return
    kb.open_scope()
    gp = kb.sb("gp", [128, 8])
    kb.dma("sp", gp[:], W("gpar"))
    nega = kb.sb("nega", [128, 4])
    kb.act(nega[:], gp[:, 0:4], AF.Exp)
    kb.ts("dve", nega[:], nega[:], -1.0, ALU.mult)
    convw = kb.sb("convw", [128, 3, 2, 5])
    kb.dma("sp", convw[:], W("convw"))
    gnorm = kb.sb("gnorm", [128, 1])
    kb.dma("sp", gnorm[:], io["gnorm"])

    def gate_prepass(gsrc, NT, tag):
        G, gc, egl = gate_core(kb, masks, gsrc, NT, tag, gp, nega)
        r = {}
        beta = kb.sb("beta" + tag, [128, 4, NT])
        kb.act(beta[:], G[:, :, 4:8].rearrange("p n c -> p c n"), AF.Sigmoid)
        lnb = kb.sb("lnb" + tag, [128, 4, NT])
        kb.act(lnb[:], beta[:], AF.Ln)
        glps = kb.bank()
        kb.mm(glps[:, 0:2 * NT], masks[:, M_SELF, :], gc[:, 0:2, :].rearrange("p c n -> p (c n)"))
        kb.mm(glps[:, 2 * NT:4 * NT], masks[:, M_SELB, :], gc[:, 2:4, :].rearrange("p c n -> p (c n)"))
        ekd = kb.sb("ekd" + tag, [128, 4, NT])
        kb.tt("dve", ekd[:].rearrange("p c n -> p (c n)"), glps[:, 0:4 * NT], gc[:].rearrange("p c n -> p (c n)"), ALU.subtract)
        kb.act(ekd[:], ekd[:], AF.Exp)
        egc = kb.sb("egc" + tag, [128, 4, NT])
        kb.act(egc[:], gc[:], AF.Exp)
        bk = kb.sb("bk" + tag, [128, 4, NT])
        kb.tt("dve", bk[:], beta[:], egc[:], ALU.mult)
        gcb = kb.sb("gcb" + tag, [128, 4, NT])
        kb.tt("dve", gcb[:], gc[:], lnb[:], ALU.add)
        r.update(beta=beta, gc=gc, gcb=gcb, bk=bk, ekd=ekd, egl=egl)
        return r

    gl = gate_prepass(gts, 32, "l")
    gx = gate_prepass(gtsc, 2, "c")
    if dbg is not None and "gc" in dbg:
        kb.dma("sp", dbg["gc"], gl["gc"][:])
        kb.dma("sp", dbg["beta"], gl["beta"][:])

    def neumann_multi(MTs, NB, res):
        W = NB * 128
        st = []
        for MTb in MTs:
            ps = kb.bank()
            for i in range(NB):
                kb.tr(ps[:, i * 128:(i + 1) * 128], MTb[:, i, :].bitcast(F32), ident[:])
            X = kb.tmp("nmX", [128, 4, 128], R32, 6)
            kb.copy("act", X[:, 0:NB, :].rearrange("p a b -> p (a b)"), ps[:, 0:W])
            TT = kb.tmp("nmT", [128, 4, 128], R32, 6)
            kb.tt("dve", TT[:, 0:NB, :], ident[:].unsqueeze(1).to_broadcast([128, NB, 128]), MTb[:, 0:NB, :], ALU.subtract)
            st.append([X, MTb, TT])
            yield
        for k in range(1, 6):
            for e in st:
                X, XT, TT = e
                ps = kb.bank()
                for i in range(NB):
                    kb.mm(ps[:, i * 128:(i + 1) * 128], XT[:, i, :], X[:, i, :])
                if k < 5:
                    psx = kb.bank()
                    for i in range(NB):
                        kb.mm(psx[:, i * 128:(i + 1) * 128], X[:, i, :], XT[:, i, :])
                Xn = kb.tmp("nmX", [128, 4, 128], R32, 6)
                kb.copy("act", Xn[:, 0:NB, :].rearrange("p a b -> p (a b)"), ps[:, 0:W])
                if k < 5:
                    XTn = kb.tmp("nmXT", [128, 4, 128], R32, 6)
                    kb.copy("dve", XTn[:, 0:NB, :].rearrange("p a b -> p (a b)"), psx[:, 0:W])
                ps2 = kb.bank()
                for i in range(NB):
                    kb.mm(ps2[:, i * 128:(i + 1) * 128], Xn[:, i, :], TT[:, i, :])
                TTn = kb.tmp("nmT", [128, 4, 128], R32, 6)
                kb.tt("dve", TTn[:, 0:NB, :].rearrange("p a b -> p (a b)"),
                      TT[:, 0:NB, :].rearrange("p a b -> p (a b)"), ps2[:, 0:W], ALU.add)
                e[0] = Xn
                e[2] = TTn
                if k < 5:
                    e[1] = XTn
                yield
        res["TTs"] = [e[2] for e in st]

    def run_interleaved(gens):
        gens = [g_ for g_ in gens if g_ is not None]
        while gens:
            for g_ in list(gens):
                try:
                    next(g_)
                except StopIteration:
                    gens.remove(g_)

    def gdn_pre(kind):
        ctx_mode = (kind == "c")
        gd = gx if ctx_mode else gl
        grp_list = [(0, L)] if ctx_mode else [(g * 512, 512) for g in range(8)]
        src = fm32c if ctx_mode else fm32
        TL = L if ctx_mode else T
        sp = spillc if ctx_mode else spill
        segs = [0, 1] if ctx_mode else [0, 1, 2]
        cs_hold = {}

        def P(t0, n, hh, cur):
            if not ctx_mode and hh == 0:
                cs = kb.tmp("cs", [128, 2, 512], F32, 2)
                kb.dma("sp", cs[:, 0, :], io["cos"][:, t0:t0 + n])
                kb.dma("sp", cs[:, 1, :], io["sin"][:, t0:t0 + n])
                cs_hold[t0] = cs
            cs = cs_hold.get(t0)
            outs = []
            for seg in segs:
                raw = kb.tmp("raw", [128, 516], F32, 3)
                lo = max(t0 - 2, 0)
                hi = min(t0 + n + 2, TL)
                d0 = lo - (t0 - 2)
                if d0 > 0:
                    kb.memset("dve", raw[:, 0:d0], 0.0)
                if hi < t0 + n + 2:
                    kb.memset("dve", raw[:, n + 4 - (t0 + n + 2 - hi):n + 4], 0.0)
                kb.dma("sp", raw[:, d0:d0 + (hi - lo)], src[seg * 2 + hh][:, lo:hi])
                cv_ = kb.tmp("cv%d" % seg, [128, 512], F32, 2)
                kb.ts("dve", cv_[:, 0:n], raw[:, 0:n], convw[:, seg, hh, 0:1], ALU.mult)
                for k in range(1, 5):
                    kb.stt("dve", cv_[:, 0:n], raw[:, k:k + n], convw[:, seg, hh, k:k + 1], cv_[:, 0:n],
                           ALU.mult, ALU.add)
                kb.act(cv_[:, 0:n], cv_[:, 0:n], AF.Silu)
                outs.append(cv_)
                yield
            kc_, vc_ = outs[0], outs[1]
            qc_ = outs[2] if not ctx_mode else None
            fin = []
            for which, xc in (("k", kc_), ("q", qc_)):
                if xc is None:
                    fin.append(None)
                    continue
                sq = kb.tmp("gsq", [128, 512], R32, 2)
                kb.act(sq[:, 0:n], xc[:, 0:n], AF.Square)
                nps = kb.bank()
                kb.mm(nps[:, 0:n], onesr[:], sq[:, 0:n])
                rn = kb.tmp("grn", [128, 512], F32, 2)
                kb.act(rn[:, 0:n], nps[:, 0:n], AF.Ln, bias=EPS)
                kb.act(rn[:, 0:n], rn[:, 0:n], AF.Exp, scale=-0.5)
                xn = kb.tmp("gxn" + which, [128, 512], R32, 2)
                if which == "q":
                    kb.stt("dve", xn[:, 0:n], xc[:, 0:n], SCALE, rn[:, 0:n], ALU.mult, ALU.mult)
                else:
                    kb.tt("dve", xn[:, 0:n], xc[:, 0:n], rn[:, 0:n], ALU.mult)
                if ctx_mode:
                    fin.append(xn)
                    continue
                rps = kb.bank()
                kb.mm(rps[:, 0:n], masksr[:, M_ROT, :], xn[:, 0:n])
                t1 = kb.tmp("grt1", [128, 512], F32, 2)
                kb.tt("dve", t1[:, 0:n], xn[:, 0:n], cs[:, 0, 0:n], ALU.mult)
                t2 = kb.tmp("grt2", [128, 512], F32, 2)
                kb.tt("dve", t2[:, 0:n], rps[:, 0:n], cs[:, 1, 0:n], ALU.mult)
                xr = kb.tmp("gxr" + which, [128, 512], R32, 2)
                kb.tt("dve", xr[:, 0:n], t1[:, 0:n], t2[:, 0:n], ALU.add)
                fin.append(xr)
                yield
            KT, QT = fin
            if dbg is not None and "KT" in dbg and not ctx_mode and t0 == 0:
                kb.dma("sp", dbg["KT"][hh], KT[:])
                kb.dma("sp", dbg["QT"][hh], QT[:])
                kb.dma("sp", dbg["VT"][hh], vc_[:])
            NB = n // 128
            tn0 = t0 // 128
            MTb = [kb.tmp("gMTb%d" % d, [128, 4, 128], R32, 2) for d in range(2)]
            kbgb = [kb.tmp("gkbgb%d" % d, [128, 4, 128], R32, 2) for d in range(2)]
            vbeb = [kb.tmp("gvbeb%d" % d, [128, 4, 128], R32, 2) for d in range(2)]
            stgb = [kb.tmp("gstgb%d" % d, [128, 4, 5, 128], F32, 2) for d in range(2)]
            Wd = NB * 128
            v3 = lambda ap: ap.rearrange("p (a b) -> p a b", b=128)
            tpK = kb.bank()
            tpV = kb.bank()
            for s in range(NB):
                sl = slice(s * 128, (s + 1) * 128)
                kb.tr(tpK[:, sl], KT[:, sl].bitcast(F32), ident[:])
                kb.tr(tpV[:, sl], vc_[:, sl], ident[:])
            for d in range(2):
                c4 = d * 2 + hh
                sc = lambda nm: gd[nm][:, c4, tn0:tn0 + NB].unsqueeze(2).to_broadcast([128, NB, 128])
                kb.tt("dve", kbgb[d][:, 0:NB, :], v3(tpK[:, 0:Wd]), sc("bk"), ALU.mult)
                kb.tt("dve", stgb[d][:, 0:NB, 4, :], v3(tpK[:, 0:Wd]), sc("ekd"), ALU.mult)
                kb.tt("dve", vbeb[d][:, 0:NB, :], v3(tpV[:, 0:Wd]), sc("beta"), ALU.mult)
            yield
            kkK = kb.bank()
            for s in range(NB):
                sl = slice(s * 128, (s + 1) * 128)
                kb.mm(kkK[:, sl], KT[:, sl], KT[:, sl])
            if not ctx_mode:
                kkQ = kb.bank()
                for s in range(NB):
                    sl = slice(s * 128, (s + 1) * 128)
                    kb.mm(kkQ[:, sl], KT[:, sl], QT[:, sl])
            combos = [(d, kind) for d in range(2) for kind in ((1,) if ctx_mode else (0, 1))]
            scf = lambda d, nm: gd[nm][:, d * 2 + hh, tn0:tn0 + NB].unsqueeze(2).to_broadcast([128, NB, 128])
            dgts, bcs, Lts, DDs = {}, {}, {}, {}
            for (d, kind) in combos:
                dgt = kb.tmp("gdg", [128, 4, 128], R32, 2)
                kb.tt("dve", dgt[:, 0:NB, :], ident[:].unsqueeze(1).to_broadcast([128, NB, 128]),
                      scf(d, "gc" if kind == 0 else "gcb"), ALU.mult)
                bc = kb.bank()
                kb.mm(bc[:, 0:Wd], onesr[:], dgt[:, 0:NB, :].rearrange("p a b -> p (a b)"))
                bcs[(d, kind)] = bc
            for (d, kind) in combos:
                Lt = kb.tmp("gLt", [128, 4, 128], F32, 4)
                kb.tt("dve", Lt[:, 0:NB, :], v3(bcs[(d, kind)][:, 0:Wd]), scf(d, "gc"), ALU.subtract)
                if kind == 0:
                    mi = M_AIF if d == 0 else M_AIB
                else:
                    mi = M_ASF if d == 0 else M_ASB
                kb.tt("dve", Lt[:, 0:NB, :], Lt[:, 0:NB, :], masks[:, mi, :].unsqueeze(1).to_broadcast([128, NB, 128]), ALU.add)
                Lts[(d, kind)] = Lt
            ebcs = {}
            for (d, kind) in combos:
                DDt = Lts[(d, kind)]
                kb.act(DDt[:, 0:NB, :], DDt[:, 0:NB, :], AF.Exp)
                DDs[(d, kind)] = DDt
                if kind == 0:
                    ebc = kb.tmp("gebc", [128, 4, 128], F32, 2)
                    kb.act(ebc[:, 0:NB, :], v3(bcs[(d, kind)][:, 0:Wd]), AF.Exp)
                    ebcs[d] = ebc
            for (d, kind) in combos:
                if kind == 1:
                    kb.tt("dve", MTb[d][:, 0:NB, :], v3(kkK[:, 0:Wd]), DDs[(d, kind)][:, 0:NB, :], ALU.mult)
                else:
                    kb.tt("dve", stgb[d][:, 0:NB, 2, :], v3(kkQ[:, 0:Wd]), DDs[(d, kind)][:, 0:NB, :], ALU.mult)
                    kb.tt("dve", stgb[d][:, 0:NB, 1, :], v3(QT[:, 0:Wd]), ebcs[d][:, 0:NB, :], ALU.mult)
            yield

            cur.update(MTb=MTb, kbgb=kbgb, vbeb=vbeb, stgb=stgb, NB=NB, tn0=tn0, hh=hh)

        def N(cur):
            MTb, kbgb, vbeb, stgb, NB, tn0, hh = (cur[k_] for k_ in ('MTb', 'kbgb', 'vbeb', 'stgb', 'NB', 'tn0', 'hh'))
            res = {}
            yield from neumann_multi(MTb, NB, res)
            TTs = res['TTs']
            for d in range(2):
                c4 = d * 2 + hh
                ups = kb.bank()
                wps = kb.bank()
                for s in range(NB):
                    kb.mm(ups[:, s * 128:(s + 1) * 128], TTs[d][:, s, :], vbeb[d][:, s, :])
                    kb.mm(wps[:, s * 128:(s + 1) * 128], kbgb[d][:, s, :], TTs[d][:, s, :])
                kb.copy("act", stgb[d][:, 0:NB, 3, :], ups[:, 0:NB * 128].rearrange("p (a b) -> p a b", b=128))
                kb.copy("dve", stgb[d][:, 0:NB, 0, :], wps[:, 0:NB * 128].rearrange("p (a b) -> p a b", b=128))
                for s in range(NB):
                    if ctx_mode:
                        kb.dma("pool", sp[c4, tn0 + s, 0].unsqueeze(1), stgb[d][:, s, 0:1, :])
                        kb.dma("pool", sp[c4, tn0 + s, 3:5].rearrange("o t k -> t o k"), stgb[d][:, s, 3:5, :])
                    else:
                        kb.dma("pool", sp[c4, tn0 + s].rearrange("o t k -> t o k"), stgb[d][:, s, :, :])
                yield


        prev = None
        for (t0, n) in grp_list:
            for hh in range(2):
                cur = {}
                run_interleaved([N(prev) if prev is not None else None, P(t0, n, hh, cur)])
                prev = cur
        run_interleaved([N(prev)])

    gdn_pre("c")
    gdn_pre("l")
    kb.close_scope()

    if dbg is not None and "spill" in dbg:
        kb.open_scope()
        for c4 in range(4):
            t_ = kb.tmp("dbgs", [128, 5, 128], F32, 2)
            kb.dma("sp", t_[:], spill[c4, 0].rearrange("o t k -> t o k"))
            kb.dma("sp", dbg["spill"][c4].rearrange("o t k -> t o k"), t_[:])
        kb.close_scope()

    if STOP == "A3":
        return
    if do_scan:
        phase_a_scan(nc, kb, io, sh, [(gidx, g)], [ymix], dbg=dbg)


def phase_a_scan(nc, kb, io, sh, glist, ymix_dsts, dbg=None):
    ident, masks, onesr = sh["ident"], sh["masks"], sh["onesr"]
    NGR = len(glist)
    kb.open_scope()
    gnorm = kb.sb("gnorm2", [128, 1])
    kb.dma("sp", gnorm[:], io["gnorm"])
    egl_l, egl_c, oacc, S = {}, {}, {}, {}
    for (gi, g) in glist:
        gp = kb.sb("gp2_%d" % gi, [128, 8])
        kb.dma("sp", gp[:], io["gpar"] if g is None else io["gpar"][g])
        nega = kb.sb("nega2_%d" % gi, [128, 4])
        kb.act(nega[:], gp[:, 0:4], AF.Exp)
        kb.ts("dve", nega[:], nega[:], -1.0, ALU.mult)
        egl_l[gi] = gate_core(kb, masks, sh["gts"][gi], 32, "ls%d" % gi, gp, nega)[2]
        egl_c[gi] = gate_core(kb, masks, sh["gtsc"][gi], 2, "cs%d" % gi, gp, nega)[2]
        oacc[gi] = kb.sb("oacc%d" % gi, [128, 2, T])
        for c4 in range(4):
            s0 = kb.sb("S%d_%da" % (gi, c4), [128, 128], R32)
            s1 = kb.sb("S%d_%db" % (gi, c4), [128, 128], R32)
            kb.ts("dve", s0[:], ident[:], 0.0, ALU.mult)
            S[(gi, c4)] = [s0, s1, 0]
    visited = set()

    def scan_step(kind, items):
        ctx_mode = (kind == "c")
        ld = {}
        for (gi, c4, tn) in items:
            ck = (gi, c4)
            sp = sh["spillc"][gi] if ctx_mode else sh["spill"][gi]
            Fm = kb.tmp("scF%d_%d" % ck, [128, 2, 128], F32, 1)
            if ctx_mode:
                kb.dma("sp", Fm[:, 0:1, :], sp[c4, tn, 0].unsqueeze(1))
            else:
                kb.dma("sp", Fm[:], sp[c4, tn, 0:2].rearrange("o t k -> t o k"))
            Tk = kb.tmp("scT%d_%d" % ck, [64, 2, 3, 128], F32, 1)
            for q in range(2):
                if ctx_mode:
                    kb.dma("sp", Tk[:, q, 1:3, :], sp[c4, tn, 3:5, q * 64:(q + 1) * 64, :].rearrange("o j k -> j o k"))
                else:
                    kb.dma("sp", Tk[:, q, :, :], sp[c4, tn, 2:5, q * 64:(q + 1) * 64, :].rearrange("o j k -> j o k"))
            Fr = kb.tmp("scFr%d_%d" % ck, [128, 2, 128], R32, 2)
            Tr = kb.tmp("scTr%d_%d" % ck, [64, 2, 2, 128], R32, 2)
            if ctx_mode:
                kb.copy("act", Fr[:, 0:1, :], Fm[:, 0:1, :])
                kb.copy("pool", Tr[:, :, 1, :], Tk[:, :, 2, :])
            else:
                kb.copy("act", Fr[:], Fm[:])
                kb.copy("pool", Tr[:, :, 0:2, :], Tk[:, :, 0:3:2, :])
            ld[ck] = (Fr, Tk, Tr)
        for qi in range(2):
            work = []
            for (gi, c4, tn) in items:
                ck = (gi, c4)
                d, hh = c4 // 2, c4 % 2
                q = qi if d == 0 else 1 - qi
                Fr, Tk, Tr = ld[ck]
                st = S[ck]
                cur, nxt = st[st[2]], st[1 - st[2]]
                st[2] = 1 - st[2]
                work.append((ck, gi, c4, tn, d, hh, q, Fr, Tk, Tr, cur, nxt))
            vpss, vn, spss, opss = {}, {}, {}, {}
            for b0 in range(0, len(work), 3):
                sub = work[b0:b0 + 3]
                for (ck, gi, c4, tn, d, hh, q, Fr, Tk, Tr, cur, nxt) in sub:
                    vps = kb.bank()
                    kb.mm(vps[0:64, 0:128], Fr[:, 0, q * 64:(q + 1) * 64], cur[:])
                    vpss[ck] = vps
                for (ck, gi, c4, tn, d, hh, q, Fr, Tk, Tr, cur, nxt) in sub:
                    vnew = kb.tmp("scv%d_%d" % ck, [64, 128], R32, 2)
                    kb.tt("dve", vnew[:], Tk[:, q, 1, :], vpss[ck][0:64, 0:128], ALU.subtract)
                    vn[ck] = vnew
                for (ck, gi, c4, tn, d, hh, q, Fr, Tk, Tr, cur, nxt) in sub:
                    sps = kb.bank()
                    kb.mm(sps[:, 0:128], Tr[:, q, 1, :], vn[ck][:])
                    spss[ck] = sps
                for (ck, gi, c4, tn, d, hh, q, Fr, Tk, Tr, cur, nxt) in sub:
                    kb.stt("dve", nxt[:], cur[:], (egl_c if ctx_mode else egl_l)[gi][:, q, c4, tn:tn + 1],
                           spss[ck][:, 0:128], ALU.mult, ALU.add)
                if ctx_mode:
                    continue
                for (ck, gi, c4, tn, d, hh, q, Fr, Tk, Tr, cur, nxt) in sub:
                    ops_ = kb.bank()
                    qs = slice(q * 64, (q + 1) * 64)
                    kb.mm(ops_[:, 0:64], cur[:], Fr[:, 1, qs], start=True, stop=False)
                    kb.mm(ops_[:, 0:64], vn[ck][:], Tr[:, q, 0, qs], start=False, stop=True)
                    opss[ck] = ops_
                for (ck, gi, c4, tn, d, hh, q, Fr, Tk, Tr, cur, nxt) in sub:
                    tok0 = tn * 128 + q * 64
                    key = (gi, hh, tok0)
                    if key not in visited:
                        visited.add(key)
                        kb.copy("act", oacc[gi][:, hh, tok0:tok0 + 64], opss[ck][:, 0:64])
                    else:
                        kb.tt("dve", oacc[gi][:, hh, tok0:tok0 + 64], oacc[gi][:, hh, tok0:tok0 + 64],
                              opss[ck][:, 0:64], ALU.add)

    for step in range(2):
        scan_step("c", [(gi, c4, step if c4 // 2 == 0 else 1 - step) for (gi, g) in glist for c4 in range(4)])
    for step in range(32):
        scan_step("l", [(gi, c4, step if c4 // 2 == 0 else 31 - step) for (gi, g) in glist for c4 in range(4)])

    if dbg is not None and "oacc" in dbg:
        kb.dma("sp", dbg["oacc"], oacc[glist[0][0]][:])

    for k, (gi, g) in enumerate(glist):
        for hh in range(2):
            for tg in range(8):
                sl = slice(tg * 512, (tg + 1) * 512)
                z = kb.tmp("oz", [128, 512], F32, 1)
                kb.dma("sp", z[:], sh["fm32"][gi][6 + hh][:, sl])
                kb.act(z[:], z[:], AF.Silu)
                sq = kb.tmp("osq", [128, 512], R32, 1)
                kb.act(sq[:], oacc[gi][:, hh, sl], AF.Square)
                nps = kb.bank()
                kb.mm(nps[:], onesr[:], sq[:])
                rn = kb.tmp("orn", [128, 512], F32, 1)
                kb.act(rn[:], nps[:], AF.Ln, bias=EPS, scale=1.0 / 128)
                kb.act(rn[:], rn[:], AF.Exp, scale=-0.5)
                t_ = kb.tmp("ot", [128, 512], F32, 1)
                kb.tt("dve", t_[:], oacc[gi][:, hh, sl], rn[:], ALU.mult)
                yo = kb.tmp("oy", [128, 512], BF16, 2)
                kb.stt("dve", yo[:], z[:], gnorm[:, 0:1], t_[:], ALU.mult, ALU.mult)
                kb.dma("sp", ymix_dsts[k][hh][:, sl], yo[:])
    kb.close_scope()


D = 2048
NCH = 16
FH = 5632
NT = 1026
EPS = 1e-6
TILES_ALL = [(0, 512), (512, 512), (1024, 2)]
TILES_MAIN = [(1, 512), (513, 512)]


def ada_io(nc):
    io = {}
    io["svec"] = nc.dram_tensor("svec", [128, NCH, 3], F32, kind="ExternalInput").ap()
    io["adaw"] = nc.dram_tensor("adaw", [D, 3072], F32, kind="ExternalInput").ap()
    io["adab"] = nc.dram_tensor("adab", [1, 3072], F32, kind="ExternalInput").ap()
    io["mod"] = nc.dram_tensor("mod", [3, 3072], F32, kind="ExternalOutput").ap()
    return io


def ada_prog(nc, kb, io):
    cv = kb.sb("cv", [128, NCH, 3])
    kb.dma("sp", cv[:], io["svec"])
    sv = kb.sb("sv", [128, NCH, 3])
    kb.act(sv[:], cv[:], AF.Silu)
    bb = kb.sb("bb", [3, 3072])
    for r in range(3):
        kb.dma("sp", bb[r:r + 1, :], io["adab"])
    adaw_v = io["adaw"].rearrange("(c p) n -> p c n", p=128)
    res = kb.sb("res", [3, 3072])
    for blk in range(6):
        wa = kb.tmp("adaw", [128, NCH, 512], F32, 2)
        kb.dma("sp", wa[:], adaw_v[:, :, blk * 512:(blk + 1) * 512])
        ps = kb.bank()
        for c in range(NCH):
            kb.mm(ps[0:3, :], sv[:, c, :], wa[:, c, :], start=(c == 0), stop=(c == NCH - 1))
        kb.tt("dve", res[:, blk * 512:(blk + 1) * 512], ps[0:3, :], bb[:, blk * 512:(blk + 1) * 512], ALU.add)
    kb.dma("sp", io["mod"], res[:])


def phase_b_io(nc):
    io = {}

    def I(name, shape, dt=F32):
        io[name] = nc.dram_tensor(name, list(shape), dt, kind="ExternalInput").ap()

    I("xT", [D, NT]); I("ymixT", [D, NT], BF16); I("hmask", [128, 2]); I("modB", [128, 10, NCH])
    I("norms", [128, 4, NCH])
    I("w_out0", [D, D]); I("wg0", [D, FH]); I("wu0", [D, FH]); I("wd0", [FH, D])
    I("w_in1", [D, 3 * D]); I("conv1", [128, NCH, 3]); I("w_out1", [D, D])
    I("wg1", [D, FH]); I("wu1", [D, FH]); I("wd1", [FH, D])
    io["outT"] = nc.dram_tensor("outT", [D, 1024], F32, kind="ExternalOutput").ap()
    return io


def phase_b(nc, kb, io, fill=None):
    onesb = kb.sb("onesb", [128, 128], BF16)
    kb.memset("dve", onesb[:], 1.0)
    xT = kb.sb("xT", [128, NCH, NT])
    kb.dma("sp", xT[:], (io["xTo"] if fill is not None else io["xT"]).rearrange("(c p) t -> p c t", p=128))
    bufA = kb.sb("bufA", [128, NCH, NT], BF16)
    bufB = kb.sb("bufB", [128, NCH, NT], BF16)
    modB = kb.sb("modB", [128, 10, NCH])
    if fill is None:
        kb.dma("sp", bufA[:], io["ymixT"].rearrange("(c p) t -> p c t", p=128))
        kb.dma("sp", modB[:], io["modB"])
    else:
        fill(bufA, modB)
    norms = kb.sb("norms", [128, 4, NCH])
    kb.dma("sp", norms[:], io["norms"])
    hmask = kb.sb("hmask", [128, 2])
    kb.dma("sp", hmask[:], io["hmask"])
    conv1 = kb.sb("conv1", [128, NCH, 3])
    kb.dma("sp", conv1[:], io["conv1"])
    coefA = kb.sb("coefA", [128, 4, NCH])
    for i, (ni, sci) in enumerate(((0, 2), (1, 5), (2, 8))):
        kb.ts("dve", coefA[:, i, :], modB[:, sci, :], 1.0, ALU.add)
        kb.tt("dve", coefA[:, i, :], coefA[:, i, :], norms[:, ni, :], ALU.mult)

    def modulate(dst, tiles, Acoef, Bcoef):
        for (c0, n) in tiles:
            ssq = kb.bank()
            for c in range(NCH):
                sq = kb.tmp("sq", [128, 512], BF16, 2)
                kb.act(sq[:, 0:n], xT[:, c, c0:c0 + n], AF.Square)
                kb.mm(ssq[:, 0:n], onesb[:], sq[:, 0:n], start=(c == 0), stop=(c == NCH - 1))
            rn = kb.tmp("rn", [128, 512], F32, 1)
            kb.act(rn[:, 0:n], ssq[:, 0:n], AF.Sqrt, bias=EPS, scale=1.0 / D)
            kb.recip(rn[:, 0:n], rn[:, 0:n])
            for c in range(NCH):
                tf = kb.tmp("tf", [128, 512], F32, 2)
                kb.tt("dve", tf[:, 0:n], xT[:, c, c0:c0 + n], rn[:, 0:n], ALU.mult)
                kb.act(dst[:, c, c0:c0 + n], tf[:, 0:n], AF.Identity, bias=Bcoef[:, c:c + 1], scale=Acoef[:, c:c + 1])

    def out_proj(W, src, tiles, gain):
        wv = W.rearrange("(c p) n -> p c n", p=128)
        for grp in range(8):
            wt = kb.tmp("wo", [128, NCH, 256], BF16, 2)
            kb.dma("pool", wt[:], wv[:, :, grp * 256:(grp + 1) * 256])
            for j in range(2):
                m = grp * 2 + j
                for (c0, n) in tiles:
                    ps = kb.bank()
                    for c in range(NCH):
                        kb.mm(ps[:, 0:n], wt[:, c, j * 128:(j + 1) * 128], src[:, c, c0:c0 + n],
                              start=(c == 0), stop=(c == NCH - 1))
                    kb.stt("dve", xT[:, m, c0:c0 + n], ps[:, 0:n], gain[:, m:m + 1], xT[:, m, c0:c0 + n],
                           ALU.mult, ALU.add)

    def ffn(Wg, Wu, Wd, hsrc, tiles, gain):
        wgv = Wg.rearrange("(c p) n -> p c n", p=128)
        wuv = Wu.rearrange("(c p) n -> p c n", p=128)
        for grp in range(FH // 256):
            wg = kb.tmp("wg", [128, NCH, 256], BF16, 2)
            kb.dma("pool", wg[:], wgv[:, :, grp * 256:(grp + 1) * 256])
            wu = kb.tmp("wu", [128, NCH, 256], BF16, 2)
            kb.dma("pool", wu[:], wuv[:, :, grp * 256:(grp + 1) * 256])
            wd = kb.tmp("wd", [128, 2, D], BF16, 2)
            kb.dma("pool", wd[:], Wd[grp * 256:(grp + 1) * 256, :].rearrange("(j p) n -> p j n", p=128))
            hid = kb.tmp("hid", [128, 2, NT], BF16, 2)
            for j in range(2):
                for (c0, n) in tiles:
                    psg = kb.bank()
                    for c in range(NCH):
                        kb.mm(psg[:, 0:n], wg[:, c, j * 128:(j + 1) * 128], hsrc[:, c, c0:c0 + n],
                              start=(c == 0), stop=(c == NCH - 1))
                    psu = kb.bank()
                    for c in range(NCH):
                        kb.mm(psu[:, 0:n], wu[:, c, j * 128:(j + 1) * 128], hsrc[:, c, c0:c0 + n],
                              start=(c == 0), stop=(c == NCH - 1))
                    sg = kb.tmp("sg", [128, 512], F32, 2)
                    kb.act(sg[:, 0:n], psg[:, 0:n], AF.Silu)
                    kb.tt("dve", hid[:, j, c0:c0 + n], sg[:, 0:n], psu[:, 0:n], ALU.mult)
            for m in range(NCH):
                for (c0, n) in tiles:
                    ps = kb.bank()
                    for j in range(2):
                        kb.mm(ps[:, 0:n], wd[:, j, m * 128:(m + 1) * 128], hid[:, j, c0:c0 + n],
                              start=(j == 0), stop=(j == 1))
                    kb.stt("dve", xT[:, m, c0:c0 + n], ps[:, 0:n], gain[:, m:m + 1], xT[:, m, c0:c0 + n],
                           ALU.mult, ALU.add)

    kb.open_scope()
    out_proj(io["w_out0"], bufA, TILES_ALL, modB[:, 0, :])
    kb.close_scope()
    kb.open_scope()
    modulate(bufB, TILES_ALL, coefA[:, 0, :], modB[:, 1, :])
    ffn(io["wg0"], io["wu0"], io["wd0"], bufB, TILES_ALL, modB[:, 3, :])
    kb.close_scope()

    kb.open_scope()
    modulate(bufA, TILES_ALL, coefA[:, 1, :], modB[:, 4, :])
    w1v = io["w_in1"].rearrange("(c p) n -> p c n", p=128)
    for grp in range(8):
        ws = []
        for seg in range(3):
            wt = kb.tmp("wi%d" % seg, [128, NCH, 256], BF16, 2)
            kb.dma("pool", wt[:], w1v[:, :, seg * D + grp * 256: seg * D + (grp + 1) * 256])
            ws.append(wt)
        for j in range(2):
            m = grp * 2 + j
            u = kb.tmp("u", [128, NT], F32, 1)
            gbs = kb.tmp("gbs", [128, NT], F32, 1)
            for (c0, n) in TILES_ALL:
                pss = []
                for seg in range(3):
                    ps = kb.bank()
                    for c in range(NCH):
                        kb.mm(ps[:, 0:n], ws[seg][:, c, j * 128:(j + 1) * 128], bufA[:, c, c0:c0 + n],
                              start=(c == 0), stop=(c == NCH - 1))
                    pss.append(ps)
                gcs = kb.tmp("gcs", [128, 512], F32, 2)
                kb.copy("act", gcs[:, 0:n], pss[1][:, 0:n])
                kb.tt("dve", u[:, c0:c0 + n], gcs[:, 0:n], pss[2][:, 0:n], ALU.mult)
                kb.copy("act", gbs[:, c0:c0 + n], pss[0][:, 0:n])
            kb.tt("dve", u[:, 0:1], u[:, 0:1], hmask[:, 0:1], ALU.mult)
            kb.tt("dve", u[:, NT - 1:NT], u[:, NT - 1:NT], hmask[:, 1:2], ALU.mult)
            cvo = kb.tmp("cvo", [128, 1024], F32, 1)
            kb.ts("dve", cvo[:], u[:, 0:1024], conv1[:, m, 0:1], ALU.mult)
            kb.stt("dve", cvo[:], u[:, 1:1025], conv1[:, m, 1:2], cvo[:], ALU.mult, ALU.add)
            kb.stt("dve", cvo[:], u[:, 2:1026], conv1[:, m, 2:3], cvo[:], ALU.mult, ALU.add)
            kb.tt("dve", bufB[:, m, 1:1025], cvo[:], gbs[:, 1:1025], ALU.mult)
    kb.close_scope()
    kb.open_scope()
    out_proj(io["w_out1"], bufB, TILES_MAIN, modB[:, 6, :])
    kb.close_scope()

    kb.open_scope()
    modulate(bufA, TILES_MAIN, coefA[:, 2, :], modB[:, 7, :])
    ffn(io["wg1"], io["wu1"], io["wd1"], bufA, TILES_MAIN, modB[:, 9, :])
    kb.close_scope()

    kb.open_scope()
    for (c0, n) in TILES_MAIN:
        ssq = kb.bank()
        for c in range(NCH):
            sq = kb.tmp("sq", [128, 512], BF16, 2)
            kb.act(sq[:, 0:n], xT[:, c, c0:c0 + n], AF.Square)
            kb.mm(ssq[:, 0:n], onesb[:], sq[:, 0:n], start=(c == 0), stop=(c == NCH - 1))
        rn = kb.tmp("rn", [128, 512], F32, 1)
        kb.act(rn[:, 0:n], ssq[:, 0:n], AF.Sqrt, bias=EPS, scale=1.0 / D)
        kb.recip(rn[:, 0:n], rn[:, 0:n])
        for c in range(NCH):
            of = kb.tmp("of", [128, 512], F32, 3)
            kb.stt("dve", of[:, 0:n], xT[:, c, c0:c0 + n], norms[:, 3, c:c + 1], rn[:, 0:n], ALU.mult, ALU.mult)
            kb.dma("sp", io["outT"][c * 128:(c + 1) * 128, c0 - 1:c0 - 1 + n], of[:, 0:n])
    kb.close_scope()


def fused_io(nc):
    io = {}

    def I(name, shape, dt=F32):
        io[name] = nc.dram_tensor(name, list(shape), dt, kind="ExternalInput").ap()

    I("xT", [D, T]); I("ctxT", [D, L]); I("nmix", [128, NCH])
    I("wf", [4, D, 1536]); I("wt", [4, D, 264]); I("convw", [4, 128, 3, 2, 5]); I("gpar", [4, 128, 8])
    I("gnorm", [128, 1]); I("nabias", [4, 5, 2, 128, 640]); I("ident", [128, 128]); I("cmask", [128, NMASK, 128])
    I("cos", [128, T]); I("sin", [128, T])
    I("svec", [128, NCH, 2]); I("adaw", [2, D, 12288]); I("adab", [2, 12288]); I("i2", [2, 2])
    I("xTo", [D, NT]); I("hmask", [128, 2]); I("jsel", [128, 4]); I("norms", [128, 4, NCH])
    I("w_out0", [D, D]); I("wg0", [D, FH]); I("wu0", [D, FH]); I("wd0", [FH, D])
    I("w_in1", [D, 3 * D]); I("conv1", [128, NCH, 3]); I("w_out1", [D, D])
    I("wg1", [D, FH]); I("wu1", [D, FH]); I("wd1", [FH, D])
    io["outT"] = nc.dram_tensor("outT", [D, 1024], F32, kind="ExternalOutput").ap()
    return io


def fused_prog(nc, kb, io):
    modG = kb.sb("modG", [128, 192, 2])
    kb.open_scope()
    cv = kb.sb("cv", [128, NCH, 2])
    kb.dma("sp", cv[:], io["svec"])
    svb = kb.sb("svb", [128, NCH, 2], BF16)
    kb.act(svb[:], cv[:], AF.Silu)
    i2 = kb.sb("i2", [2, 2])
    kb.dma("sp", i2[:], io["i2"])
    rows = kb.sb("rows", [2, 2 * 12288])
    for l in range(2):
        for r in range(2):
            kb.dma("sp", rows[r:r + 1, l * 12288:(l + 1) * 12288], io["adab"][l:l + 1, :])
    for l in range(2):
        wv = io["adaw"][l].rearrange("(c p) n -> p c n", p=128)
        for blk in range(24):
            wa = kb.tmp("adaw", [128, NCH, 512], BF16, 3)
            kb.dma("pool", wa[:], wv[:, :, blk * 512:(blk + 1) * 512])
            ps = kb.bank()
            for c in range(NCH):
                kb.mm(ps[0:2, :], svb[:, c, :], wa[:, c, :], start=(c == 0), stop=(c == NCH - 1))
            o0 = l * 12288 + blk * 512
            kb.tt("dve", rows[:, o0:o0 + 512], ps[0:2, :], rows[:, o0:o0 + 512], ALU.add)
    for half in range(2):
        tps = kb.bank()
        for gch in range(96):
            gg = half * 96 + gch
            kb.mm(tps[:, gch * 2:gch * 2 + 2], rows[:, gg * 128:(gg + 1) * 128], i2[:])
        kb.copy("dve", modG[:, half * 96:(half + 1) * 96, :].rearrange("p a b -> p (a b)"), tps[:, 0:192])
    kb.close_scope()

    kb.open_outer()
    sh = phase_a_shared(nc, kb, io, ng=4)
    ymix_all = nc.dram_tensor("ymix_all", [4, 4, 128, T], BF16).ap()

    def modsrc(A1v, B1v):
        kb.open_scope()
        nmix = kb.sb("nmix", [128, NCH])
        kb.dma("sp", nmix[:], io["nmix"])
        kb.copy("dve", B1v[:], modG[:, 0:16, :])
        kb.ts("dve", A1v[:], modG[:, 16:32, :], 1.0, ALU.add)
        kb.tt("dve", A1v[:], A1v[:], nmix[:].unsqueeze(2).to_broadcast([128, NCH, 2]), ALU.mult)
        kb.close_scope()

    for g in range(4):
        phase_a(nc, kb, io, sh=sh, g=g, modsrc=(modsrc if g == 0 else "done"), ymix_dst=ymix_all[g], do_scan=False)
    for pair in ((0, 1), (2, 3)):
        phase_a_scan(nc, kb, io, sh, [(g, g) for g in pair], [ymix_all[g] for g in pair])

    kb.close_outer()

    def fill(bufA, modB):
        jsel = kb.sb("jsel", [128, 4])
        kb.dma("sp", jsel[:], io["jsel"])
        kb.open_scope()
        for j in range(4):
            cand = kb.tmp("cand", [128, NCH, NT], BF16, 1)
            lo = j * 1024 - 1
            hi = j * 1024 + 1025
            slo, shi = max(lo, 0), min(hi, T)
            d0 = slo - lo
            if d0 > 0:
                kb.memset("pool", cand[:, :, 0:d0], 0.0)
            if shi < hi:
                kb.memset("pool", cand[:, :, NT - (hi - shi):NT], 0.0)
            for ch in range(NCH):
                h = ch % 8
                gsrc, idx = h // 2, (h % 2) + (0 if ch < 8 else 2)
                kb.dma("sp", cand[:, ch, d0:d0 + (shi - slo)], ymix_all[gsrc, idx][:, slo:shi])
            if j == 0:
                kb.ts("dve", bufA[:], cand[:], jsel[:, 0:1], ALU.mult)
            else:
                kb.stt("dve", bufA[:], cand[:], jsel[:, j:j + 1], bufA[:], ALU.mult, ALU.add)
        kb.close_scope()
        for i, blk in enumerate((2, 3, 4, 5, 6, 7, 8, 9, 10, 11)):
            off = blk * 16 if i < 4 else 96 + (blk - 6) * 16
            kb.copy("dve", modB[:, i, :], modG[:, off:off + 16, 0])

    phase_b(nc, kb, io, fill=fill)


T = 4096
L = 256
EVEN_OFF = dict(ka=0, va=1024, kb=2048, vb=3072, a=4096, b=4112, qa=4128, qb=5152, za=6176)


def pc_layout(v):
    return np.ascontiguousarray(v.reshape(16, 128).T)


def make_consts():
    ident = np.eye(128, dtype=np.float32)
    t = np.arange(128)
    ch = t // 64
    same = ch[:, None] == ch[None, :]
    m = np.zeros((13, 128, 128), np.float32)
    m[0] = same & (t[:, None] <= t[None, :])
    m[1] = same & (t[:, None] >= t[None, :])
    m[2] = (t[:, None] == (64 * ch[None, :] + 63))
    m[3] = (t[:, None] == (64 * ch[None, :]))
    for k, r in enumerate((63, 127, 0, 64)):
        m[4 + k][r, :] = 1.0
    NEG = -1.0e4
    j = t[:, None]
    i = t[None, :]
    m[8] = np.where(same & (i >= j), 0.0, NEG)
    m[9] = np.where(same & (i > j), 0.0, NEG)
    m[10] = np.where(same & (i <= j), 0.0, NEG)
    m[11] = np.where(same & (i < j), 0.0, NEG)
    R = np.zeros((128, 128), np.float32)
    for dp in range(64):
        R[dp + 64, dp] = -1.0
    for dp in range(64, 128):
        R[dp - 64, dp] = 1.0
    m[12] = R
    cmask = np.ascontiguousarray(m.transpose(1, 0, 2))
    tt = np.arange(T)
    row = (tt // 64).astype(np.float32)
    col = (tt % 64).astype(np.float32)
    inv_freq = (np.float32(10000.0) ** (-np.arange(32, dtype=np.float32) / np.float32(32))).astype(np.float32)
    ang = np.concatenate([row[:, None] * inv_freq, col[:, None] * inv_freq], axis=-1).astype(np.float32)
    cos = np.cos(ang).astype(np.float32).T
    sin = np.sin(ang).astype(np.float32).T
    cos = np.ascontiguousarray(np.concatenate([cos, cos], 0))
    sin = np.ascontiguousarray(np.concatenate([sin, sin], 0))
    return dict(ident=ident, cmask=cmask, cos=cos, sin=sin)


def make_nabias(rpb_h):
    out = np.full((5, 2, 128, 640), -30000.0, np.float32)
    cq = np.arange(64)
    kc = np.arange(64)
    cs = np.clip(cq - 8, 0, 48)
    valid_col = (kc[None, :] >= cs[:, None]) & (kc[None, :] < cs[:, None] + 16)
    dc = np.clip(kc[None, :] - cq[:, None], -15, 15) + 15
    for ty, m in enumerate((0, 1, 2, 30, 31)):
        base_row = 2 * min(max(m - 2, 0), 27)
        for rr in range(2):
            r = 2 * m + rr
            rs = min(max(r - 4, 0), 56)
            for jrow in range(10):
                krow = base_row + jrow
                if not (rs <= krow < rs + 8):
                    continue
                dr = krow - r + 7
                for hh in range(2):
                    blk = np.where(valid_col, rpb_h[hh][dr][dc], np.float32(-30000.0))
                    out[ty, hh, rr * 64:(rr + 1) * 64, jrow * 64:(jrow + 1) * 64] = blk
    return out


def group_weights(inp, g):
    w_in = inp["ev_w_in"][0]
    heads = [2 * g, 2 * g + 1]

    def cols(seg, h):
        o = EVEN_OFF[seg] + h * 128
        return w_in[:, o:o + 128]

    wf = np.concatenate([cols(s, h) for s in ("ka", "va", "qa", "za", "kb", "qb") for h in heads], axis=1)
    gate_cols = []
    for seg in ("a", "b"):
        for d in range(2):
            for h in heads:
                o = EVEN_OFF[seg] + d * 8 + h
                gate_cols.append(w_in[:, o:o + 1])
    wt = np.concatenate([cols("vb", h) for h in heads] + gate_cols, axis=1)
    conv = inp["ev_conv"][0]
    convw = np.zeros((128, 3, 2, 5), np.float32)
    for seg in range(3):
        for hh, h in enumerate(heads):
            convw[:, seg, hh, :] = conv[:, seg * 1024 + h * 128: seg * 1024 + (h + 1) * 128].T
    a_log = inp["ev_a_log"][0]
    dtb = inp["ev_dt_bias"][0]
    gp = np.zeros((8,), np.float32)
    for d in range(2):
        for hh, h in enumerate(heads):
            gp[d * 2 + hh] = a_log[d, h]
            gp[4 + d * 2 + hh] = dtb[d, h]
    gpar = np.ascontiguousarray(np.broadcast_to(gp[None, :], (128, 8))).astype(np.float32)
    return dict(wf=np.ascontiguousarray(wf, dtype=np.float32), wt=np.ascontiguousarray(wt, dtype=np.float32),
                convw=convw, gpar=gpar, nabias=make_nabias(inp["ev_rpb"][0][heads]))


def host_a(inp, core, consts, mod):
    b, g = core // 4, core % 4
    modA = np.zeros((128, 2, 16, 2), np.float32)
    for blk in range(2):
        modA[:, blk, :, 0] = pc_layout(mod[b, 0, blk * 2048:(blk + 1) * 2048])
        modA[:, blk, :, 1] = pc_layout(mod[2, 0, blk * 2048:(blk + 1) * 2048])
    m = dict(
        xT=np.ascontiguousarray(inp["x"][b].T),
        ctxT=np.ascontiguousarray(inp["ctx"][b].T),
        modA=modA,
        nmix=pc_layout(inp["norm_mix"][0]),
        gnorm=np.ascontiguousarray(inp["ev_gdn_norm"][0].reshape(128, 1)),
    )
    m.update(group_weights(inp, g))
    m.update(consts)
    return {k: np.ascontiguousarray(v, dtype=np.float32) for k, v in m.items()}


def host_f(inp, core, consts, gw_all, xT_b, ctxT_b):
    b, j = core // 4, core % 4
    t0 = j * 1024
    x = inp["x"][b]
    xTo = np.zeros((2048, 1026), np.float32)
    xTo[:, 1:1025] = x[t0:t0 + 1024].T
    hmask = np.zeros((128, 2), np.float32)
    if t0 > 0:
        xTo[:, 0] = x[t0 - 1]
        hmask[:, 0] = 1.0
    if t0 + 1024 < 4096:
        xTo[:, 1025] = x[t0 + 1024]
        hmask[:, 1] = 1.0
    jsel = np.zeros((128, 4), np.float32)
    jsel[:, j] = 1.0
    norms = np.stack([pc_layout(inp["norm_ffn"][0]), pc_layout(inp["norm_mix"][1]),
                      pc_layout(inp["norm_ffn"][1]), pc_layout(inp["final_norm"])], axis=1)
    conv1 = np.stack([pc_layout(inp["od_conv"][0][k]) for k in range(3)], axis=-1)
    svec = np.stack([pc_layout(inp["c"][b]), pc_layout(inp["c_ctx"])], axis=-1)
    f = lambda a: np.ascontiguousarray(a, dtype=np.float32)
    m = dict(
        xT=xT_b[b], ctxT=ctxT_b[b], nmix=pc_layout(inp["norm_mix"][0]),
        gnorm=f(inp["ev_gdn_norm"][0].reshape(128, 1)),
        svec=f(svec), adaw=f(inp["ada_w"]), adab=f(inp["ada_b"]), i2=np.eye(2, dtype=np.float32),
        xTo=f(xTo), hmask=f(hmask), jsel=f(jsel), norms=f(norms),
        w_out0=f(inp["ev_w_out"][0]), wg0=f(inp["ffn_w_gate"][0]), wu0=f(inp["ffn_w_up"][0]), wd0=f(inp["ffn_w_down"][0]),
        w_in1=f(inp["od_w_in"][0]), conv1=f(conv1), w_out1=f(inp["od_w_out"][0]),
        wg1=f(inp["ffn_w_gate"][1]), wu1=f(inp["ffn_w_up"][1]), wd1=f(inp["ffn_w_down"][1]),
    )
    m.update(gw_all)
    m.update(consts)
    return {k: np.ascontiguousarray(v, dtype=np.float32) for k, v in m.items()}


D = 2048


def host_ada(inp, core):
    l, off = divmod(core * 3072, 12288)
    svec = np.stack([pc_layout(inp["c"][0]), pc_layout(inp["c"][1]), pc_layout(inp["c_ctx"])], axis=-1)
    return dict(
        svec=np.ascontiguousarray(svec, dtype=np.float32),
        adaw=np.ascontiguousarray(inp["ada_w"][l][:, off:off + 3072]),
        adab=np.ascontiguousarray(inp["ada_b"][l][off:off + 3072].reshape(1, 3072)),
    )


def assemble_mod(res_list):
    full = np.concatenate([np.asarray(r["mod"], dtype=np.float32) for r in res_list], axis=1)
    return full.reshape(3, 2, 12288)


def mod_pc(v):
    return pc_layout(v)


def host_b(inp, core, mod, ymix_all):
    b, j = core // 4, core % 4
    t0 = j * 1024
    x = inp["x"][b]
    xT = np.zeros((D, 1026), np.float32)
    xT[:, 1:1025] = x[t0:t0 + 1024].T
    hmask = np.zeros((128, 2), np.float32)
    ym = np.zeros((D, 1026), ml_dtypes.bfloat16)

    def ycols(lo, hi):
        out = np.zeros((D, hi - lo), ml_dtypes.bfloat16)
        for g in range(4):
            r = ymix_all[b * 4 + g]
            for hh in range(2):
                h = 2 * g + hh
                out[h * 128:(h + 1) * 128] = r[hh][:, lo:hi]
                out[1024 + h * 128:1024 + (h + 1) * 128] = r[2 + hh][:, lo:hi]
        return out

    ym[:, 1:1025] = ycols(t0, t0 + 1024)
    if t0 > 0:
        xT[:, 0] = x[t0 - 1]
        ym[:, 0:1] = ycols(t0 - 1, t0)
        hmask[:, 0] = 1.0
    if t0 + 1024 < 4096:
        xT[:, 1025] = x[t0 + 1024]
        ym[:, 1025:1026] = ycols(t0 + 1024, t0 + 1025)
        hmask[:, 1] = 1.0
    m0 = mod[b, 0].reshape(6, D)
    m1 = mod[b, 1].reshape(6, D)
    sel = [m0[2], m0[3], m0[4], m0[5], m1[0], m1[1], m1[2], m1[3], m1[4], m1[5]]
    modB = np.stack([pc_layout(v) for v in sel], axis=1)
    norms = np.stack([pc_layout(inp["norm_ffn"][0]), pc_layout(inp["norm_mix"][1]),
                      pc_layout(inp["norm_ffn"][1]), pc_layout(inp["final_norm"])], axis=1)
    conv1 = np.stack([pc_layout(inp["od_conv"][0][k]) for k in range(3)], axis=-1)
    f = lambda a: np.ascontiguousarray(a, dtype=np.float32)
    return dict(
        xT=f(xT), ymixT=np.ascontiguousarray(ym), hmask=f(hmask), modB=f(modB), norms=f(norms),
        w_out0=f(inp["ev_w_out"][0]), wg0=f(inp["ffn_w_gate"][0]), wu0=f(inp["ffn_w_up"][0]), wd0=f(inp["ffn_w_down"][0]),
        w_in1=f(inp["od_w_in"][0]), conv1=f(conv1), w_out1=f(inp["od_w_out"][0]),
        wg1=f(inp["ffn_w_gate"][1]), wu1=f(inp["ffn_w_up"][1]), wd1=f(inp["ffn_w_down"][1]),
    )

def _build(io_fn, prog_fn):
    nc = bass.Bass("TRN2", target_bir_lowering=False)
    io = io_fn(nc)
    with ExitStack() as es:
        kb = KB(nc, es)
        prog_fn(nc, kb, io)
        kb.p.emit()
    return nc


def kernel(**inputs):
    inp = {k: np.asarray(v) for k, v in inputs.items()}
    cores = list(range(8))
    consts = make_consts()
    gws = [group_weights(inp, g) for g in range(4)]
    gw_all = {k: np.stack([gw[k] for gw in gws]) for k in gws[0]}
    xT_b = [np.ascontiguousarray(inp["x"][b].T) for b in range(2)]
    ctxT_b = [np.ascontiguousarray(inp["ctx"][b].T) for b in range(2)]
    nc = _build(fused_io, fused_prog)
    in_maps = [host_f(inp, k, consts, gw_all, xT_b, ctxT_b) for k in cores]
    res = run_bass_kernel_spmd(nc, in_maps, core_ids=cores)
    out = np.zeros((2, 4096, 2048), np.float32)
    for k in cores:
        b, j = k // 4, k % 4
        out[b, j * 1024:(j + 1) * 1024, :] = np.asarray(res.results[k]["outT"]).T
    return out
```
